# Optimizing a Trainium2 kernel written in Bass

```python
import jax, jax.numpy as jnp
from jax import lax
import numpy as np

D_MODEL = 1024
BATCH = 8
SEQ = 8192
DEPTH = 1

N_META = 16
FOURIER_GROUPS = 4
FOURIER_GROUP_DIM = 128
FOURIER_WIDTH = FOURIER_GROUPS * FOURIER_GROUP_DIM
CONV_HEADS = 8
CONV_HEAD_DIM = 64
CONV_WIDTH = CONV_HEADS * CONV_HEAD_DIM
CONV_TAPS = 3
N_BRANCHES = 2
D_FF = 2816
RMS_EPS = 1e-6
IN_COLS = FOURIER_WIDTH + 3 * CONV_WIDTH + N_BRANCHES * D_MODEL

kernel_name = "hybrid_fourier_shortconv_encoder_block"


def rmsnorm(x, g):
    xf = x.astype(jnp.float32)
    var = jnp.mean(xf * xf, axis=-1, keepdims=True)
    return (xf * lax.rsqrt(var + RMS_EPS) * g.astype(jnp.float32)).astype(x.dtype)


def centred_dwconv(x, w, b):
    half = CONV_TAPS // 2
    L = x.shape[1]
    xp = jnp.pad(x, ((0, 0), (half, half), (0, 0)))
    y = b
    for k in range(CONV_TAPS):
        y = y + xp[:, k:k + L] * w[k]
    return y


def fourier_mix(u):
    bn, L, _ = u.shape
    ug = u.astype(jnp.float32).reshape(bn, L, FOURIER_GROUPS, FOURIER_GROUP_DIM)
    f = jnp.fft.fft2(ug, axes=(1, 3), norm="ortho").real
    return f.reshape(bn, L, FOURIER_WIDTH).astype(u.dtype)


def setup_inputs(seed: int = 0) -> dict:
    key = jax.random.key(seed)
    ks = jax.random.split(key, 20)
    f32 = jnp.float32

    def normal(k, shape, fan_in):
        return jax.random.normal(k, shape, f32) * (fan_in ** -0.5)

    def gain(k, shape):
        return 1.0 + 0.02 * jax.random.normal(k, shape, f32)

    def bias(k, shape):
        return 0.01 * jax.random.normal(k, shape, f32)

    return {
        "x": jax.random.normal(ks[0], (BATCH, SEQ, D_MODEL), f32),
        "meta_tokens": jax.random.normal(ks[1], (N_META, D_MODEL), f32),
        "g_mix_pre": gain(ks[2], (DEPTH, D_MODEL)),
        "w_in": normal(ks[3], (DEPTH, D_MODEL, IN_COLS), D_MODEL),
        "b_gates": bias(ks[4], (DEPTH, N_BRANCHES * D_MODEL)),
        "w_fourier": normal(ks[5], (DEPTH, FOURIER_WIDTH, D_MODEL), FOURIER_WIDTH),
        "conv_w_mix": normal(ks[6], (DEPTH, CONV_TAPS, CONV_WIDTH), CONV_TAPS),
        "conv_b_mix": bias(ks[7], (DEPTH, CONV_WIDTH)),
        "w_conv_out": normal(ks[8], (DEPTH, CONV_WIDTH, D_MODEL), CONV_WIDTH),
        "w_out": normal(ks[9], (DEPTH, D_MODEL, D_MODEL), D_MODEL),
        "g_mix_post": gain(ks[10], (DEPTH, D_MODEL)),
        "g_ffn_pre": gain(ks[11], (DEPTH, D_MODEL)),
        "w_up": normal(ks[12], (DEPTH, D_MODEL, 2 * D_FF), D_MODEL),
        "conv_w_ffn": normal(ks[13], (DEPTH, CONV_TAPS, 2 * D_FF), CONV_TAPS),
        "conv_b_ffn": bias(ks[14], (DEPTH, 2 * D_FF)),
        "w_down": normal(ks[15], (DEPTH, D_FF, D_MODEL), D_FF),
        "g_ffn_post": gain(ks[16], (DEPTH, D_MODEL)),
    }


def reference(x, meta_tokens, g_mix_pre, w_in, b_gates, w_fourier, conv_w_mix, conv_b_mix,
              w_conv_out, w_out, g_mix_post, g_ffn_pre, w_up, conv_w_ffn, conv_b_ffn,
              w_down, g_ffn_post):
    bn = x.shape[0]
    meta = jnp.broadcast_to(meta_tokens[None].astype(x.dtype), (bn, N_META, D_MODEL))
    h_res = jnp.concatenate([meta, x], axis=1)
    L = h_res.shape[1]

    splits = [FOURIER_WIDTH,
              FOURIER_WIDTH + CONV_WIDTH,
              FOURIER_WIDTH + 2 * CONV_WIDTH,
              FOURIER_WIDTH + 3 * CONV_WIDTH]

    for l in range(DEPTH):
        h = rmsnorm(h_res, g_mix_pre[l])
        proj = h @ w_in[l]
        u_a, c_g, b_g, v_b, gate_logits = jnp.split(proj, splits, axis=-1)

        y_a = fourier_mix(u_a) @ w_fourier[l]

        conv_in = c_g * v_b
        y_b = (b_g * centred_dwconv(conv_in, conv_w_mix[l], conv_b_mix[l])) @ w_conv_out[l]

        gates = jax.nn.sigmoid((gate_logits + b_gates[l]).reshape(bn, L, N_BRANCHES, D_MODEL))
        merged = gates[:, :, 0] * y_a + gates[:, :, 1] * y_b
        mix_out = merged @ w_out[l]
        h_res = h_res + rmsnorm(mix_out, g_mix_post[l])

        h2 = rmsnorm(h_res, g_ffn_pre[l])
        up = centred_dwconv(h2 @ w_up[l], conv_w_ffn[l], conv_b_ffn[l])
        a, v = jnp.split(up, 2, axis=-1)
        ffn_out = (jax.nn.gelu(a) * v) @ w_down[l]
        h_res = h_res + rmsnorm(ffn_out, g_ffn_post[l])

    return h_res[:, N_META:]
```

```python
import contextlib
import numpy as np
import ml_dtypes
import concourse.bass as bass
import concourse.mybir as mybir
from concourse.bass_utils import run_bass_kernel_spmd

F32 = mybir.dt.float32
BF16 = mybir.dt.bfloat16
AF = mybir.ActivationFunctionType
ALU = mybir.AluOpType

D = 1024
SEQ = 8192
NMETA = 16
L = SEQ + NMETA
LH = L // 2
NT = 33
NPAD = NT * 128
DFF = 2816
NKB = 9
EPS = 1e-6

V_G1, V_GP1, V_G2, V_GP2 = 0, 8, 16, 24
V_BG = 32
V_CWM = 48
V_CBM = 60
V_CWF = 64
V_CBF = 196
NV = 240

ENGS = ("sync", "scalar", "vector", "gpsimd", "tensor")


class Op:
    __slots__ = ("eng", "fn", "deps", "is_dma", "sig", "signaling", "dsem", "seq")

    def __init__(self, eng, fn, is_dma, dsem):
        self.eng = eng
        self.fn = fn
        self.deps = []
        self.is_dma = is_dma
        self.sig = None
        self.signaling = is_dma
        self.dsem = dsem


class Prog:
    def __init__(self):
        self.ops = {e: [] for e in ENGS}
        self.last_w = {}
        self.readers = {}
        self.same_engine_sync = {"scalar", "vector", "gpsimd"}
        self.dma_sems = {}
        self.group_sems = set()
        self.nops = 0

    def op(self, eng, fn, reads=(), writes=(), dma=False, dsem=None, group=False):
        o = Op(eng, fn, dma, dsem)
        self.nops += 1
        o.seq = self.nops
        if dma:
            assert self.dma_sems.setdefault(dsem, eng) == eng
            if group:
                self.group_sems.add(dsem)
        deps = {}
        for t in reads:
            w = self.last_w.get(t)
            if w is not None:
                deps[id(w)] = w
        for t in writes:
            w = self.last_w.get(t)
            if w is not None:
                deps[id(w)] = w
            for r in self.readers.get(t, ()):
                deps[id(r)] = r
        best = {}
        for p in deps.values():
            if p is o:
                continue
            if (not p.is_dma) and (not dma) and p.eng == eng and eng not in self.same_engine_sync:
                continue
            key = ("dma", p.dsem) if p.is_dma else ("eng", p.eng)
            q = best.get(key)
            if q is None or p.seq > q.seq:
                best[key] = p
        for p in best.values():
            o.deps.append(p)
            p.signaling = True
        for t in reads:
            self.readers.setdefault(t, []).append(o)
        for t in writes:
            self.last_w[t] = o
            self.readers[t] = []
        self.ops[eng].append(o)
        return o

    def emit(self, nc, final_wait_ops=()):
        dma_cnt = {}
        for e in ENGS:
            c = 0
            for o in self.ops[e]:
                if o.is_dma:
                    dma_cnt[o.dsem] = dma_cnt.get(o.dsem, 0) + 16
                    o.sig = (("dma", o.dsem), dma_cnt[o.dsem])
                elif o.signaling:
                    c += 1
                    o.sig = (("eng", e), c)
        for e in ENGS:
            for o in self.ops[e]:
                if o.is_dma and o.dsem in self.group_sems:
                    o.sig = (("dma", o.dsem), dma_cnt[o.dsem])
        semkeys = [("eng", e) for e in ENGS] + [("dma", d) for d in sorted(self.dma_sems)]
        with contextlib.ExitStack() as st:
            sems = {k: st.enter_context(nc.semaphore(f"s_{k[0]}_{k[1]}")) for k in semkeys}
            block = st.enter_context(nc.Block())
            stats = {}

            def make_body(e):
                def body(engobj):
                    seen = {}
                    nw = 0
                    for o in self.ops[e]:
                        for p in o.deps:
                            k, v = p.sig
                            if seen.get(k, 0) >= v:
                                continue
                            seen[k] = v
                            engobj.wait_ge(sems[k], v)
                            nw += 1
                        inst = o.fn(engobj)
                        if o.signaling:
                            inst.then_inc(sems[o.sig[0]], 16 if o.is_dma else 1)
                    if e == "sync":
                        for o in final_wait_ops:
                            k, v = o.sig
                            if seen.get(k, 0) >= v:
                                continue
                            seen[k] = v
                            engobj.wait_ge(sems[k], v)
                    stats[e] = (len(self.ops[e]), nw)
                return body

            for e in ENGS:
                getattr(block, e)(make_body(e))
            self.stats = stats


def build_program(debug=None):
    nc = bass.Bass("TRN2", target_bir_lowering=False)
    P = Prog()

    def dram_in(name, shape, dt):
        return nc.dram_tensor(name, shape, dt, kind="ExternalInput").ap()

    x_d = dram_in("x", [SEQ, D], F32)
    meta_d = dram_in("meta", [NMETA, D], F32)
    w_in_d = dram_in("w_in", [D, 4096], F32)
    w_f_d = dram_in("w_f", [512, D], F32)
    w_co_d = dram_in("w_co", [512, D], F32)
    w_out_d = dram_in("w_out", [D, D], F32)
    w_up_d = dram_in("w_up", [D, 2 * DFF], F32)
    w_dn_d = dram_in("w_dn", [DFF, D], F32)
    vecs_d = dram_in("vecs", [128, NV], F32)
    ident_d = dram_in("ident", [128, 128], F32)
    cs_d = dram_in("cs128", [128, 256], BF16)
    dft_d = dram_in("dft", [NKB * 11 * 128, 3072], BF16)
    out_d = nc.dram_tensor("out", [SEQ, D], F32, kind="ExternalOutput").ap()

    s_in = nc.dram_tensor("s_in", [D, 4096], BF16, kind="Internal").ap()
    s_f = nc.dram_tensor("s_f", [512, D], BF16, kind="Internal").ap()
    s_co = nc.dram_tensor("s_co", [512, D], BF16, kind="Internal").ap()
    s_out = nc.dram_tensor("s_out", [D, D], BF16, kind="Internal").ap()
    s_up = nc.dram_tensor("s_up", [D, 2 * DFF], BF16, kind="Internal").ap()
    s_dn = nc.dram_tensor("s_dn", [DFF, D], BF16, kind="Internal").ap()

    NSLOT = 3
    with contextlib.ExitStack() as st:
        def sb(name, shape, dt):
            return st.enter_context(nc.sbuf_tensor(name, shape, dt))

        def ps(name, shape, dt):
            return st.enter_context(nc.psum_tensor(name, shape, dt))

        VEC = sb("VEC", [128, NV], F32)
        ID = sb("ID", [128, 128], F32)
        CS = sb("CS", [128, 256], BF16)
        ONES = sb("ONES", [128, 128], BF16)
        EPSC = sb("EPSC", [128, 2], F32)
        DUM = sb("DUM", [128, 16], F32)
        WR = sb("WR", [128, NSLOT, 4096], BF16)
        R1 = sb("R1", [128, 33792], BF16)
        R2 = sb("R2", [128, 33792], BF16)
        RS = sb("RS", [128, 8, 514], F32)
        XS2 = sb("XS", [128, 2, 1024], F32)
        RS2 = sb("RS2", [128, 8, 514], F32)
        SQ = sb("SQ", [128, 2, 512], BF16)
        RSTD = sb("RSTD", [128, 512], F32)
        RT = sb("RT", [128, 512], F32)
        UCAR = sb("UCAR", [128, 44, 2], F32)
        CAR_CV = sb("CAR_CV", [128, 4, 2], F32)
        CAR_BG = sb("CAR_BG", [128, 4, 2], BF16)
        CAR_GA = sb("CAR_GA", [128, 8, 2], BF16)
        CAR_GB = sb("CAR_GB", [128, 8, 2], BF16)

        PB = [ps(f"PB{i}", [128, 512], F32) for i in range(6)]
        TRT = ps("TRT", [128, 1024], F32)
        banks = [(PB[i][:, :], f"P{i}") for i in range(6)] + [(TRT[:, 0:512], "P6"), (TRT[:, 512:1024], "P7")]

        def view(reg, byte_off, dt, shape):
            n = int(np.prod(shape[1:]))
            if dt == F32:
                v = reg[:, byte_off // 2: byte_off // 2 + 2 * n].bitcast(F32)
            else:
                v = reg[:, byte_off // 2: byte_off // 2 + n]
            if len(shape) == 3:
                v = v.rearrange("p (a b) -> p a b", a=shape[1])
            return v

        UT = view(R1, 0, BF16, [128, 4, L])
        UCS = view(R1, 0, BF16, [128, NT, 1024])
        UP = view(R2, 0, BF16, [128, 4, NPAD])
        UM = view(R2, 2 * 4 * NPAD, BF16, [128, 4, NPAD])
        FT = view(R2, 0, BF16, [128, 4, L])
        G = view(R1, 0, BF16, [128, 22, 512])
        GA = view(R1, 0, BF16, [128, 8, 514])
        GB = view(R1, 8224, BF16, [128, 8, 514])
        M = view(R1, 22528, BF16, [128, 8, 512])
        S0 = 30720
        CV = view(R1, S0, F32, [128, 4, 514])
        BG = view(R1, S0 + 8224, BF16, [128, 4, 514])
        CO = view(R1, S0 + 12336, F32, [128, 2, 512])
        Z = view(R1, S0 + 16432, BF16, [128, 4, 512])
        T1 = view(R1, S0 + 20528, F32, [128, 2, 512])
        T2 = view(R1, S0 + 24624, F32, [128, 2, 512])
        MO = view(R1, S0, F32, [128, 8, 512])
        U = view(R1, S0, F32, [128, 6, 516])
        ACC = view(R1, S0 + 12384, F32, [128, 6, 512])
        GL = view(R1, S0 + 24672, F32, [128, 2, 512])
        OSS = [(view(R1, S0 + 16384, F32, [128, 1024]), ["OS0"]), (view(R1, S0 + 20480, F32, [128, 1024]), ["OS1"])]
        SQB = view(R1, 63488, BF16, [128, 2, 512])
        RSTD2 = view(R1, 65536, F32, [128, 512])
        HTM = (M, "M")
        SQS = [SQ, SQB]
        RSTDS = [(RSTD, "RSTD"), (RSTD2, "RSTD2")]
        RT2 = view(R1, S0 + 28768, F32, [128, 512])
        RTS = [(RT, "RT"), (RT2, "RT2")]

        ASB = RS[:, 0:2, 0:512]
        xsconf = {"buf": XS2, "n": 2}
        cnt = {"bank": 0, "slot": 0, "alt": 0, "fence": 0, "xs": 0, "os": 0}

        def act(out, in_, func, reads, writes, scale=1.0, bias=None):
            kw = {} if bias is None else {"bias": bias}
            return P.op("scalar", lambda e: e.activation(out=out, in_=in_, func=func, scale=scale, **kw),
                        reads=reads, writes=writes)

        def stt(out, in0, scalar, in1, op0, op1, reads, writes):
            return P.op("vector", lambda e: e.scalar_tensor_tensor(out=out, in0=in0, scalar=scalar, in1=in1,
                                                                   op0=op0, op1=op1), reads=reads, writes=writes)

        def tt(eng, out, in0, in1, op, reads, writes):
            return P.op(eng, lambda e: e.tensor_tensor(out=out, in0=in0, in1=in1, op=op), reads=reads, writes=writes)

        def cp(eng, out, in_, reads, writes):
            if eng == "scalar":
                return P.op(eng, lambda e: e.copy(out=out, in_=in_), reads=reads, writes=writes)
            return P.op(eng, lambda e: e.tensor_copy(out=out, in_=in_), reads=reads, writes=writes)

        def mset(eng, ap, val, writes, reads=()):
            return P.op(eng, lambda e: e.memset(ap, val), reads=reads, writes=writes)

        def mm(out, lhsT, rhs, start, stop, reads, writes):
            return P.op("tensor", lambda e: e.matmul(out, lhsT=lhsT, rhs=rhs, start=start, stop=stop),
                        reads=reads, writes=writes)

        def tr(out, in_, ident, reads, writes):
            return P.op("tensor", lambda e: e.transpose(out=out, in_=in_, identity=ident), reads=reads, writes=writes)

        def dma(q, out, in_, dsem, reads, writes, group=False):
            return P.op(q, lambda e: e.dma_start(out=out, in_=in_), reads=reads, writes=writes, dma=True,
                        dsem=dsem, group=group)

        def fence(reads):
            cnt["fence"] += 1
            tok = f"fence{cnt['fence']}"
            k = cnt["fence"] % 16
            mset("gpsimd", DUM[:, k:k + 1], 0.0, writes=[tok, f"DUM{k}"] + list(reads))
            return tok

        def next_bank():
            b = cnt["bank"] % 4
            cnt["bank"] += 1
            return banks[b]

        SSBS = [banks[4], banks[5]]
        TRA = (TRT[:, :], ["P6", "P7"])

        def alt_eng():
            cnt["alt"] += 1
            return "scalar" if cnt["alt"] % 2 else "vector"

        dbg_outs = {}

        def dbg(name, ap, shape, dt, reads):
            if debug and name in debug:
                t = nc.dram_tensor("dbg_" + name, shape, dt, kind="ExternalOutput").ap()
                o = dma("sync", t, ap, "dbg_" + name, reads=reads, writes=[])
                dbg_outs[name] = o

        def load_unit(src, kc, ranges, src_tok):
            slot = cnt["slot"] % NSLOT
            cnt["slot"] += 1
            tot = sum(n for _, n in ranges)
            tok = f"W{slot}"
            v = WR[:, slot, 0:kc * tot].rearrange("p (k n) -> p k n", k=kc)
            if len(ranges) == 1:
                c0, n = ranges[0]
                s3 = src.rearrange("(k p) n -> p k n", p=128)
                dma("sync", v, s3[:, :, c0:c0 + n], f"wr{slot}", reads=[src_tok], writes=[tok])
            else:
                (c0, n), (c1, n1) = ranges
                assert n == n1
                s4 = src.rearrange("(k p) (h n) -> p k h n", p=128, h=2)
                assert c1 - c0 == s4.shape[3]
                v4 = WR[:, slot, 0:kc * tot].rearrange("p (k h n) -> p k h n", k=kc, h=2)
                dma("sync", v4, s4[:, :, :, c0:c0 + n], f"wr{slot}", reads=[src_tok], writes=[tok])
            return v, [tok]

        dma("sync", VEC[:, :], vecs_d[:, :], "c0", reads=[], writes=["VEC"])
        dma("sync", ID[:, :], ident_d[:, :], "c1", reads=[], writes=["ID"])
        dma("sync", CS[:, :], cs_d[:, :], "c2", reads=[], writes=["CS"])
        mset("gpsimd", ONES[:, :], 1.0, writes=["ONES"])
        mset("gpsimd", EPSC[:, :], EPS, writes=["EPSC"])
        mset("gpsimd", UCAR[:, :, :], 0.0, writes=[f"UCAR{c}" for c in range(44)])

        def cast_weight(src, dst, rows, cols, cpiece, name):
            for r0 in range(0, rows, 128):
                for c0 in range(0, cols, cpiece):
                    dma("gpsimd", dst[r0:r0 + 128, c0:c0 + cpiece], src[r0:r0 + 128, c0:c0 + cpiece],
                        "cast_" + name, reads=[], writes=["SCR_" + name + f"_{r0}_{c0}"], group=True)
            return "SCR_" + name + f"_{rows - 128}_{cols - cpiece}"

        T_IN0 = cast_weight(w_in_d[:, 0:512], s_in[:, 0:512], D, 512, 512, "in0")
        T_IN = cast_weight(w_in_d[:, 512:4096], s_in[:, 512:4096], D, 3584, 1792, "in")
        T_F = cast_weight(w_f_d, s_f, 512, D, 1024, "f")
        T_CO = cast_weight(w_co_d, s_co, 512, D, 1024, "co")
        T_OUT = cast_weight(w_out_d, s_out, D, D, 1024, "out")
        s_up_r = s_up.rearrange("r (u h j) -> r u h j", h=2, j=256)
        for r0 in range(0, D, 128):
            for h in range(2):
                dma("gpsimd", s_up_r[r0:r0 + 128, :, h, :],
                    w_up_d[r0:r0 + 128, DFF * h:DFF * (h + 1)].rearrange("r (u j) -> r u j", j=256),
                    "cast_up", reads=[], writes=[f"SCR_up_{r0}_{h}"], group=True)
        T_UP = f"SCR_up_{D - 128}_1"
        T_DN = cast_weight(w_dn_d, s_dn, DFF, D, 1024, "dn")

        def load_x_sub(col0, sub, r, ta, rsb):
            rs, rsn = rsb
            XS, nxs = xsconf["buf"], xsconf["n"]
            slot = cnt["xs"] % nxs
            cnt["xs"] += 1
            tok = f"XS{slot}"
            if ta < NMETA:
                nm = min(NMETA - ta, r)
                dma("sync", XS[0:nm, slot, :], meta_d[ta:ta + nm, :], f"xs{slot}", reads=[], writes=[tok])
                if r > nm:
                    dma("sync", XS[nm:r, slot, :], x_d[0:r - nm, :], f"xs{slot}", reads=[], writes=[tok])
            else:
                dma("sync", XS[0:r, slot, :], x_d[ta - NMETA:ta - NMETA + r, :], f"xs{slot}", reads=[], writes=[tok])
            for dc in range(8):
                tr(TRT[:, 128 * dc:128 * dc + r], XS[0:r, slot, 128 * dc:128 * dc + 128], ID[0:r, 0:r],
                   reads=[tok, "ID"], writes=TRA[1])
            src = TRT[:, :].rearrange("p (c n) -> p c n", c=8)[:, :, 0:r]
            cp("scalar", rs[:, :, col0 + 128 * sub:col0 + 128 * sub + r], src, reads=[],
               writes=TRA[1] + [f"{rsn}{dc}" for dc in range(8)])

        def load_x_T(col0, n, t0, rsb):
            for sub in range((n + 127) // 128):
                load_x_sub(col0, sub, min(128, n - 128 * sub), t0 + 128 * sub, rsb)

        def finish_rstd(n, which=0):
            ssb = SSBS[which]
            rstd, rtok = RSTDS[which]
            rt, rttok = RTS[which]
            act(rt[:, 0:n], ssb[0][:, 0:n], AF.Sqrt, reads=["EPSC"], writes=[ssb[1], rttok], scale=1.0 / D,
                bias=EPSC[:, 0:1])
            P.op("vector", lambda e: e.reciprocal(out=rstd[:, 0:n], in_=rt[:, 0:n]), reads=[rttok], writes=[rtok])

        def norm_part1(colA, n, rsb, which=0, extra=()):
            rs, rsn = rsb
            ssb = SSBS[which]
            for dc in range(8):
                sq = SQS[which][:, dc % 2, 0:n]
                sqt = f"SQ{which}{dc % 2}"
                act(sq, rs[:, dc, colA:colA + n], AF.Square, reads=[f"{rsn}{dc}"] + list(extra), writes=[sqt])
                mm(ssb[0][:, 0:n], ONES[:, :], sq, dc == 0, dc == 7, reads=[sqt, "ONES"], writes=[ssb[1]])

        def norm_part2(colA, n, gcol, rsb, htb, which=0, extra=(), do_finish=True):
            rs, rsn = rsb
            ht, htn = htb
            rstd, rtok = RSTDS[which]
            if do_finish:
                finish_rstd(n, which)
            for dc in range(8):
                stt(ht[:, dc, 0:n], rs[:, dc, colA:colA + n], VEC[:, gcol + dc:gcol + dc + 1], rstd[:, 0:n],
                    ALU.mult, ALU.mult, reads=[f"{rsn}{dc}", "VEC", rtok] + list(extra), writes=[f"{htn}{dc}"])

        def norm_to_HT(colA, n, gcol, rsb, htb, which=0, extra=()):
            norm_part1(colA, n, rsb, which, extra)
            norm_part2(colA, n, gcol, rsb, htb, which, extra)

        HT_ALL = [f"HT{dc}" for dc in range(8)]

        WU, WU_T = load_unit(s_in, 8, [(0, 512)], T_IN0)
        HTA = view(R2, 0, BF16, [128, 8, 512])
        HTB = view(R2, 8192, BF16, [128, 8, 512])
        p1buf = [((RS, "RS"), (HTA, "HTa")), ((RS2, "RQ"), (HTB, "HTb"))]
        XS8 = view(R2, 16384, F32, [128, 8, 1024])
        xsconf["buf"], xsconf["n"] = XS8, 8

        def p1_width(tb):
            return 512 if tb < 16 else 16

        sv = (SQS[1], RSTDS[1], RTS[1])
        SQS[1] = view(R2, 49152, BF16, [128, 2, 512])
        RSTDS[1] = (view(R2, 51200, F32, [128, 512]), "RSTD2")
        RTS[1] = (view(R2, 53248, F32, [128, 512]), "RT2")
        load_x_T(2, p1_width(0), 0, rsb=p1buf[0][0])
        norm_part1(2, p1_width(0), p1buf[0][0], 0)
        finish_rstd(p1_width(0), 0)
        for tb in range(17):
            n = p1_width(tb)
            t0 = 512 * tb
            rsb, htb = p1buf[tb % 2]
            if tb + 1 < 17:
                load_x_T(2, p1_width(tb + 1), 512 * (tb + 1), rsb=p1buf[(tb + 1) % 2][0])
            norm_part2(2, n, V_G1, rsb, htb, tb % 2, do_finish=False)
            for g in range(4):
                bk, btok = next_bank()
                for dc in range(8):
                    mm(bk[:, 0:n], WU[:, dc, 128 * g:128 * g + 128], htb[0][:, dc, 0:n], dc == 0, dc == 7,
                       reads=WU_T + [f"{htb[1]}{dc}"], writes=[btok])
                cp(alt_eng(), UT[:, g, t0:t0 + n], bk[:, 0:n], reads=[], writes=[btok, f"UT{g}"])
            if tb + 1 < 17:
                norm_part1(2, p1_width(tb + 1), p1buf[(tb + 1) % 2][0], (tb + 1) % 2)
                finish_rstd(p1_width(tb + 1), (tb + 1) % 2)
        f_p1 = fence([f"HTa{dc}" for dc in range(8)] + [f"HTb{dc}" for dc in range(8)] + [f"XS{i}" for i in range(8)]
                     + ["SQ10", "SQ11", "RSTD2", "RT2"])
        SQS[1], RSTDS[1], RTS[1] = sv
        xsconf["buf"], xsconf["n"] = XS2, 2
        cnt["xs"] = 0
        dbg("ut", UT[:, :, :], [128, 4, L], BF16, reads=[f"UT{g}" for g in range(4)])

        for g in range(4):
            mset("gpsimd", UP[:, g, LH + 1:NPAD], 0.0, writes=[f"UP{g}"], reads=[f_p1])
            mset("gpsimd", UM[:, g, LH:NPAD], 0.0, writes=[f"UM{g}"], reads=[f_p1])
            mset("gpsimd", UM[:, g, 0:1], 0.0, writes=[f"UM{g}"])
            a = UT[:, g, 1:LH]
            b = UT[:, g, L - 1:LH:-1]
            tt("vector", UP[:, g, 1:LH], a, b, ALU.add, reads=[f"UT{g}"], writes=[f"UP{g}"])
            tt("vector", UM[:, g, 1:LH], a, b, ALU.subtract, reads=[f"UT{g}"], writes=[f"UM{g}"])
            cp("gpsimd", UP[:, g, 0:1], UT[:, g, 0:1], reads=[f"UT{g}"], writes=[f"UP{g}"])
            cp("gpsimd", UP[:, g, LH:LH + 1], UT[:, g, LH:LH + 1], reads=[f"UT{g}"], writes=[f"UP{g}"])
        f_r1 = fence([f"UT{g}" for g in range(4)])
        for i in range(NT):
            for g in range(4):
                mm(TRT[:, 256 * g:256 * g + 128], UP[:, g, 128 * i:128 * i + 128], CS[:, 0:128], True, True,
                   reads=[f"UP{g}", "CS"], writes=TRA[1])
                mm(TRT[:, 256 * g + 128:256 * g + 256], UM[:, g, 128 * i:128 * i + 128], CS[:, 128:256], True, True,
                   reads=[f"UM{g}", "CS"], writes=TRA[1])
            cp(alt_eng(), UCS[:, i, :], TRT[:, :], reads=[f_r1], writes=TRA[1] + [f"UCS{i}"])
        dbg("ucs", UCS[:, :, :], [128, NT, 1024], BF16, reads=[f"UCS{i}" for i in range(NT)])
        f_r2 = fence([f"UP{g}" for g in range(4)] + [f"UM{g}" for g in range(4)])

        it = 0
        for gp in range(2):
            for kb in range(NKB):
                wk = 512 if kb < NKB - 1 else LH + 1 - 512 * (NKB - 1)
                k0 = 512 * kb
                bset = [banks[(it % 2) * 4 + q] for q in range(4)]
                it += 1
                for t in range(11):
                    slot = cnt["slot"] % NSLOT
                    cnt["slot"] += 1
                    row0 = (kb * 11 + t) * 128
                    dv = WR[:, slot, 0:3072]
                    dtoks = [f"W{slot}"]
                    dma("sync", dv, dft_d[row0:row0 + 128, :], f"wr{slot}", reads=[], writes=dtoks)
                    dv4 = dv.rearrange("p (j c k) -> p j c k", j=3, c=2)
                    for j in range(3):
                        i = 3 * t + j
                        for gi in range(2):
                            g = 2 * gp + gi
                            mm(bset[2 * gi][0][:, 0:wk], UCS[:, i, 256 * g:256 * g + 128], dv4[:, j, 0, 0:wk],
                               i == 0, i == NT - 1, reads=dtoks + [f"UCS{i}"], writes=[bset[2 * gi][1]])
                            mm(bset[2 * gi + 1][0][:, 0:wk], UCS[:, i, 256 * g + 128:256 * g + 256],
                               dv4[:, j, 1, 0:wk], i == 0, i == NT - 1, reads=dtoks + [f"UCS{i}"],
                               writes=[bset[2 * gi + 1][1]])
                for gi in range(2):
                    g = 2 * gp + gi
                    a_ps, a_tok = bset[2 * gi]
                    b_ps, b_tok = bset[2 * gi + 1]
                    asb = ASB[:, gi, 0:wk]
                    cp("scalar", asb, a_ps[:, 0:wk], reads=[], writes=[a_tok, f"RS{gi}"])
                    tt("vector", FT[:, g, k0:k0 + wk], asb, b_ps[:, 0:wk], ALU.add, reads=[f"RS{gi}", f_r2],
                       writes=[b_tok, f"FT{g}"])
                    lo = 1 if kb == 0 else 0
                    hi = wk if kb < NKB - 1 else wk - 1
                    m = hi - lo
                    dst = FT[:, g, L - (k0 + lo):L - (k0 + lo) - m:-1]
                    tt("vector", dst, asb[:, lo:hi], b_ps[:, lo:hi], ALU.subtract, reads=[f"RS{gi}", f_r2],
                       writes=[b_tok, f"FT{g}"])
        dbg("ft", FT[:, :, :], [128, 4, L], BF16, reads=[f"FT{g}" for g in range(4)])
        f_p2 = fence([f"UCS{i}" for i in range(NT)])
        FT_ALL = [f"FT{g}" for g in range(4)]

        rsbufs = [(RS, "RS"), (RS2, "RQ")]

        def rs_of(j):
            return rsbufs[j % 2]

        CVT = [f"CV{q}" for q in range(4)]
        BGT = [f"BG{q}" for q in range(4)]
        GAT = [f"GA{q}" for q in range(8)]
        GBT = [f"GB{q}" for q in range(8)]
        MOT = [f"MO{q}" for q in range(8)]
        GT = [f"G{c}" for c in range(22)]
        U_VCB = (2, 0, 1)
        U_GATES = (3, 4, 5, 6)

        def A_proj(n, units, fA):
            HT = M
            for u in units:
                wv, wtok = load_unit(s_in, 8, [(512 + 512 * u, 512)], T_IN)
                for q in range(4):
                    bk, btok = next_bank()
                    for dc in range(8):
                        mm(bk[:, 0:n], wv[:, dc, 128 * q:128 * q + 128], HT[:, dc, 0:n], dc == 0, dc == 7,
                           reads=wtok + [f"M{dc}"], writes=[btok])
                    if u == 2:
                        cp("scalar", CV[:, q, 2:2 + n], bk[:, 0:n], reads=[fA], writes=[btok, f"CV{q}"])
                    elif u == 0:
                        tt("vector", CV[:, q, 2:2 + n], bk[:, 0:n], CV[:, q, 2:2 + n], ALU.mult, reads=[fA],
                           writes=[btok, f"CV{q}"])
                    elif u == 1:
                        cp("scalar", BG[:, q, 2:2 + n], bk[:, 0:n], reads=[fA], writes=[btok, f"BG{q}"])
                    else:
                        gi = (u - 3) * 4 + q
                        dst = GA if gi < 8 else GB
                        nm = "GA" if gi < 8 else "GB"
                        act(dst[:, gi % 8, 2:2 + n], bk[:, 0:n], AF.Sigmoid, reads=[fA, "VEC"],
                            writes=[btok, f"{nm}{gi % 8}"], bias=VEC[:, V_BG + gi:V_BG + gi + 1])

        def save_carries(n):
            cp("gpsimd", CAR_CV[:, :, :], CV[:, :, n:n + 2], reads=CVT, writes=["CAR_CV"])
            cp("gpsimd", CAR_BG[:, :, :], BG[:, :, n:n + 2], reads=BGT, writes=["CAR_BG"])
            cp("gpsimd", CAR_GA[:, :, :], GA[:, :, n:n + 2], reads=GAT, writes=["CAR_GA"])
            cp("gpsimd", CAR_GB[:, :, :], GB[:, :, n:n + 2], reads=GBT, writes=["CAR_GB"])

        def restore_gates(fA):
            cp("gpsimd", GA[:, :, 0:2], CAR_GA[:, :, :], reads=["CAR_GA", fA], writes=GAT)
            cp("gpsimd", GB[:, :, 0:2], CAR_GB[:, :, :], reads=["CAR_GB", fA], writes=GBT)

        def restore_vcb(fA):
            cp("gpsimd", CV[:, :, 0:2], CAR_CV[:, :, :], reads=["CAR_CV", fA], writes=CVT)
            cp("gpsimd", BG[:, :, 0:2], CAR_BG[:, :, :], reads=["CAR_BG", fA], writes=BGT)

        def pn_matmuls(n, produce, f_mo, which=0):
            ssb = SSBS[which]
            cntd = 0
            for dco, bk, btok in produce():
                cp("scalar", MO[:, dco, 0:n], bk[:, 0:n], reads=[f_mo], writes=[btok, f"MO{dco}"])
                sq = SQS[which][:, cntd % 2, 0:n]
                sqt = f"SQ{which}{cntd % 2}"
                act(sq, bk[:, 0:n], AF.Square, reads=[], writes=[btok, sqt])
                mm(ssb[0][:, 0:n], ONES[:, :], sq, cntd == 0, cntd == 7, reads=[sqt, "ONES"], writes=[ssb[1]])
                cntd += 1

        def pn_finish(n, gcol, rsb, colR, add_into_rs, which=0):
            rs, rsn = rsb
            rstd, rtok = RSTDS[which]
            finish_rstd(n, which)
            for dc in range(8):
                stt(MO[:, dc, 0:n], MO[:, dc, 0:n], VEC[:, gcol + dc:gcol + dc + 1], rstd[:, 0:n], ALU.mult, ALU.mult,
                    reads=["VEC", rtok], writes=[f"MO{dc}"])
                if add_into_rs:
                    tt("gpsimd", rs[:, dc, colR:colR + n], rs[:, dc, colR:colR + n], MO[:, dc, 0:n], ALU.add,
                       reads=[f"MO{dc}"], writes=[f"{rsn}{dc}"])
                else:
                    tt("gpsimd", MO[:, dc, 0:n], MO[:, dc, 0:n], rs[:, dc, colR:colR + n], ALU.add,
                       reads=[f"{rsn}{dc}"], writes=[f"MO{dc}"])

        def stage_B(nB, s, rsb, fS):
            cwm = lambda k, q: VEC[:, V_CWM + 4 * k + q:V_CWM + 4 * k + q + 1]
            for q in range(4):
                co = CO[:, q % 2, 0:nB]
                ctok = f"CO{q % 2}"
                act(co, CV[:, q, 1:1 + nB], AF.Identity, reads=[f"CV{q}", "VEC", fS], writes=[ctok], scale=cwm(1, q),
                    bias=VEC[:, V_CBM + q:V_CBM + q + 1])
                stt(co, CV[:, q, 0:nB], cwm(0, q), co, ALU.mult, ALU.add, reads=[f"CV{q}", "VEC"], writes=[ctok])
                stt(co, CV[:, q, 2:2 + nB], cwm(2, q), co, ALU.mult, ALU.add, reads=[f"CV{q}", "VEC"], writes=[ctok])
                tt("gpsimd", Z[:, q, 0:nB], BG[:, q, 1:1 + nB], co, ALU.mult, reads=[f"BG{q}", ctok, fS],
                   writes=[f"Z{q}"])
            tb0 = s - 1
            for h in range(2):
                wco, wco_t = load_unit(s_co, 4, [(512 * h, 512)], T_CO)
                wf, wf_t = load_unit(s_f, 4, [(512 * h, 512)], T_F)
                for dq in range(4):
                    dc = 4 * h + dq
                    bkb, btb = next_bank()
                    for q in range(4):
                        mm(bkb[:, 0:nB], wco[:, q, 128 * dq:128 * dq + 128], Z[:, q, 0:nB], q == 0, q == 3,
                           reads=wco_t + [f"Z{q}"], writes=[btb])
                    bka, bta = next_bank()
                    for g in range(4):
                        mm(bka[:, 0:nB], wf[:, g, 128 * dq:128 * dq + 128], FT[:, g, tb0:tb0 + nB], g == 0, g == 3,
                           reads=wf_t + [f"FT{g}"], writes=[bta])
                    t1 = T1[:, dc % 2, 0:nB]
                    t2 = T2[:, dc % 2, 0:nB]
                    tt("vector", t1, bka[:, 0:nB], GA[:, dc, 1:1 + nB], ALU.mult, reads=[f"GA{dc}", fS],
                       writes=[bta, f"T1{dc % 2}"])
                    tt("vector", t2, bkb[:, 0:nB], GB[:, dc, 1:1 + nB], ALU.mult, reads=[f"GB{dc}", fS],
                       writes=[btb, f"T2{dc % 2}"])
                    tt("gpsimd", M[:, dc, 0:nB], t1, t2, ALU.add, reads=[f"T1{dc % 2}", f"T2{dc % 2}"],
                       writes=[f"M{dc}"])
            f1 = fence(CVT + BGT + GAT + GBT + [f"Z{q}" for q in range(4)] + ["CO0", "CO1", "T10", "T11", "T20", "T21"])

            def produce_out():
                for h in range(2):
                    wv, wtok = load_unit(s_out, 8, [(512 * h, 512)], T_OUT)
                    for dq in range(4):
                        bk, btok = next_bank()
                        for k in range(8):
                            mm(bk[:, 0:nB], wv[:, k, 128 * dq:128 * dq + 128], M[:, k, 0:nB], k == 0, k == 7,
                               reads=wtok + [f"M{k}"], writes=[btok])
                        yield 4 * h + dq, bk, btok
            pn_matmuls(nB, produce_out, f1, 0)
            pn_finish(nB, V_GP1, rsb, 1, True, 0)
            f2 = fence(MOT)
            return f1, f2

        def stage_CD(nB, nD, f1, f2, epilogue, rsb, hooks):
            HT = M
            norm_to_HT(1, nB, V_G2, rsb, HTM, 0)
            cwf = lambda k, c: VEC[:, V_CWF + 44 * k + c:V_CWF + 44 * k + c + 1]

            def conv_center(c):
                for half in range(2):
                    ch = c + 22 * half
                    ui = (2 * c + half) % 6
                    ut = [f"U{ui}c", f"U{ui}d"]
                    act(ACC[:, ui, 0:nD], U[:, ui, 1:1 + nD], AF.Identity, reads=ut + ["VEC", f2], writes=[f"ACC{ui}"],
                        scale=cwf(1, ch), bias=VEC[:, V_CBF + ch:V_CBF + ch + 1])

            def conv_side(c):
                for half in range(2):
                    ch = c + 22 * half
                    ui = (2 * c + half) % 6
                    ut = [f"U{ui}c", f"U{ui}d"]
                    atok = f"ACC{ui}"
                    acc = ACC[:, ui, 0:nD]
                    stt(acc, U[:, ui, 0:nD], cwf(0, ch), acc, ALU.mult, ALU.add, reads=ut + ["VEC"], writes=[atok])
                    stt(acc, U[:, ui, 2:2 + nD], cwf(2, ch), acc, ALU.mult, ALU.add, reads=ut + ["VEC"], writes=[atok])

            def gelu_mul(c):
                a0 = (2 * c) % 6
                a1 = (2 * c + 1) % 6
                gl = GL[:, c % 2, 0:nD]
                gtok = f"GL{c % 2}"
                act(gl, ACC[:, a0, 0:nD], AF.Gelu_apprx_tanh, reads=[f"ACC{a0}", f2], writes=[gtok])
                tt("gpsimd", G[:, c, 0:nD], gl, ACC[:, a1, 0:nD], ALU.mult, reads=[gtok, f"ACC{a1}", f1],
                   writes=[f"G{c}"])

            def mm_pair(c, wv, wtok, p):
                res = []
                for half in range(2):
                    bk, btok = next_bank()
                    lo = 256 * half + 128 * p
                    for dc in range(8):
                        mm(bk[:, 0:nB], wv[:, dc, lo:lo + 128], HT[:, dc, 0:nB], dc == 0, dc == 7,
                           reads=wtok + [f"M{dc}"], writes=[btok])
                    res.append((bk, btok))
                return res

            def evac_pair(c, res):
                for half in range(2):
                    ch = c + 22 * half
                    bk, btok = res[half]
                    ui = (2 * c + half) % 6
                    cp("gpsimd", U[:, ui, 0:2], UCAR[:, ch, :], reads=[f"UCAR{ch}", f2], writes=[f"U{ui}c"])
                    if epilogue:
                        mset("gpsimd", U[:, ui, 2 + nB:4 + nB], 0.0, writes=[f"U{ui}d"], reads=[f2])
                    cp("scalar", U[:, ui, 2:2 + nB], bk[:, 0:nB], reads=[f2], writes=[btok, f"U{ui}d"])
                    if not epilogue:
                        cp("gpsimd", UCAR[:, ch, :], U[:, ui, nB:nB + 2], reads=[f"U{ui}d"], writes=[f"UCAR{ch}"])

            for u in range(11):
                wv, wtok = load_unit(s_up, 8, [(512 * u, 512)], T_UP)
                for p in range(2):
                    c = 2 * u + p
                    res = mm_pair(c, wv, wtok, p)
                    if c in hooks:
                        hooks[c]()
                    if c >= 1:
                        conv_center(c - 1)
                        conv_side(c - 1)
                    evac_pair(c, res)
                    if c >= 2:
                        gelu_mul(c - 2)
            conv_center(21)
            conv_side(21)
            gelu_mul(20)
            gelu_mul(21)
            f3 = fence([f"U{i}{x}" for i in range(6) for x in "cd"] + [f"ACC{i}" for i in range(6)] + ["GL0", "GL1"])
            return f3

        def D2a(nD, f3):
            def produce_dn():
                for pq in range(4):
                    bks = [next_bank(), next_bank()]
                    for kh in range(2):
                        wv, wtok = load_unit(s_dn[1408 * kh:1408 * (kh + 1), :], 11, [(256 * pq, 256)], T_DN)
                        for d2 in range(2):
                            for k in range(11):
                                kk = 11 * kh + k
                                mm(bks[d2][0][:, 0:nD], wv[:, k, 128 * d2:128 * d2 + 128], G[:, kk, 0:nD],
                                   kh == 0 and k == 0, kh == 1 and k == 10, reads=wtok + [f"G{kk}"],
                                   writes=[bks[d2][1]])
                    for d2 in range(2):
                        yield 2 * pq + d2, bks[d2][0], bks[d2][1]
            pn_matmuls(nD, produce_dn, f3, 0)

        def out_stage(nD, s, f3):
            tD0 = s - 2
            k0 = max(0, NMETA - tD0)
            while k0 < nD:
                r = min(128, nD - k0)
                for dc in range(8):
                    tr(TRT[0:r, 128 * dc:128 * dc + 128], MO[:, dc, k0:k0 + r], ID[:, :], reads=[f"MO{dc}", "ID"],
                       writes=TRA[1])
                oi = cnt["os"] % 2
                osb, ostoks = OSS[oi]
                cnt["os"] += 1
                cp("scalar", osb[0:r, :], TRT[0:r, :], reads=[f3], writes=TRA[1] + ostoks)
                row0 = tD0 + k0 - NMETA
                o = dma("gpsimd", out_d[row0:row0 + r, :], osb[0:r, :], f"outd{oi}", reads=ostoks, writes=[])
                out_ops.append(o)
                k0 += r

        out_ops = []

        def carry_rs(src, dst, n):
            cp("gpsimd", dst[0][:, :, 0:2], src[0][:, :, n:n + 2], reads=[f"{src[1]}{q}" for q in range(8)],
               writes=[f"{dst[1]}{q}" for q in range(8)])

        rp = rs_of(-1)
        load_x_T(2, 2, 14, rp)
        norm_to_HT(2, 2, V_G1, rp, HTM, 0, extra=[f_p2])
        A_proj(2, U_VCB + U_GATES, f_p2)
        save_carries(2)
        carry_rs(rp, rs_of(0), 2)
        load_x_T(2, 512, NMETA, rs_of(0))
        norm_to_HT(2, 512, V_G1, rs_of(0), HTM, 0)
        restore_gates(f_p2)
        restore_vcb(f_p2)
        A_proj(512, U_VCB + U_GATES, f_p2)
        save_carries(512)
        fS = f_p2
        for j in range(16):
            s = NMETA + 512 * j
            rsb = rs_of(j)
            nxt = j < 15
            f1, f2 = stage_B(512, s, rsb, fS)
            hooks = {}
            if nxt:
                for k in range(4):
                    hooks[3 + 4 * k] = (lambda k=k, j=j: load_x_sub(2, k, 128, NMETA + 512 * (j + 1) + 128 * k,
                                                                    rs_of(j + 1)))
                hooks[19] = (lambda j=j: norm_part1(2, 512, rs_of(j + 1), 1))
            f3 = stage_CD(512, 512, f1, f2, False, rsb, hooks)
            carry_rs(rsb, rs_of(j + 1), 512)
            if nxt:
                norm_part2(2, 512, V_G1, rs_of(j + 1), HTM, 1)
            D2a(512, f3)
            pn_finish(512, V_GP2, rsb, 0, False, 0)
            fG = fence(GT)
            if nxt:
                restore_gates(fG)
                A_proj(512, U_GATES, fG)
            out_stage(512, s, f3)
            fS = fence(MOT + ["OS0", "OS1"])
            if nxt:
                restore_vcb(fS)
                A_proj(512, U_VCB, fS)
                save_carries(512)
            if j == 0:
                dbg("rs0", RS[:, :, :], [128, 8, 514], F32, reads=[f"RS{q}" for q in range(8)])
        s = L
        rsb = rs_of(16)
        restore_gates(fS)
        restore_vcb(fS)
        for q in range(4):
            mset("gpsimd", CV[:, q, 2:3], 0.0, writes=[f"CV{q}"])
        f1, f2 = stage_B(1, s, rsb, fS)
        f3 = stage_CD(1, 2, f1, f2, True, rsb, {})
        D2a(2, f3)
        pn_finish(2, V_GP2, rsb, 0, False, 0)
        out_stage(2, s, f3)

        finals = list(out_ops[-2:]) + list(dbg_outs.values())
        P.emit(nc, final_wait_ops=finals)
    return nc, P


_CACHE = {}


def _consts():
    if "c" in _CACHE:
        return _CACHE["c"]
    bf = ml_dtypes.bfloat16
    c = np.arange(128)
    ang = 2 * np.pi * np.outer(c, c) / 128.0
    cs = np.concatenate([np.cos(ang), np.sin(ang)], axis=1) / np.sqrt(128.0)
    cs128 = cs.astype(np.float32).astype(bf)
    ident = np.eye(128, dtype=np.float32)
    n = np.arange(NPAD, dtype=np.int64)[:, None]
    k = np.arange(NKB * 512, dtype=np.int64)[None, :]
    m = (n * k) % L
    angp = (2 * np.pi / L) * m.astype(np.float64)
    valid = ((n <= LH) & (k <= LH))
    sc = 1.0 / np.sqrt(float(L))
    cm = np.where(valid, np.cos(angp) * sc, 0.0).astype(np.float32).astype(bf)
    sm = np.where(valid, -np.sin(angp) * sc, 0.0).astype(np.float32).astype(bf)
    del angp, m
    cm6 = cm.reshape(11, 3, 128, NKB, 512)
    sm6 = sm.reshape(11, 3, 128, NKB, 512)
    both = np.stack([cm6, sm6], axis=0)
    dft = np.ascontiguousarray(both.transpose(4, 1, 3, 2, 0, 5)).reshape(NKB * 11 * 128, 3072)
    _CACHE["c"] = (cs128, ident, dft)
    return _CACHE["c"]


def _vecs(g_mix_pre, g_mix_post, g_ffn_pre, g_ffn_post, b_gates, conv_w_mix, conv_b_mix, conv_w_ffn, conv_b_ffn):
    v = np.zeros((128, NV), np.float32)

    def colmajor(a):
        return np.asarray(a, np.float32).reshape(-1, 128).T

    v[:, V_G1:V_G1 + 8] = colmajor(g_mix_pre[0])
    v[:, V_GP1:V_GP1 + 8] = colmajor(g_mix_post[0])
    v[:, V_G2:V_G2 + 8] = colmajor(g_ffn_pre[0])
    v[:, V_GP2:V_GP2 + 8] = colmajor(g_ffn_post[0])
    v[:, V_BG:V_BG + 16] = colmajor(b_gates[0])
    for k in range(3):
        v[:, V_CWM + 4 * k:V_CWM + 4 * k + 4] = colmajor(conv_w_mix[0, k])
        v[:, V_CWF + 44 * k:V_CWF + 44 * k + 44] = colmajor(conv_w_ffn[0, k])
    v[:, V_CBM:V_CBM + 4] = colmajor(conv_b_mix[0])
    v[:, V_CBF:V_CBF + 44] = colmajor(conv_b_ffn[0])
    return v


def make_in_maps(inputs):
    cs128, ident, dft = _consts()
    f = lambda a: np.ascontiguousarray(np.asarray(a, dtype=np.float32))
    vecs = _vecs(*[np.asarray(inputs[k], np.float32) for k in
                   ("g_mix_pre", "g_mix_post", "g_ffn_pre", "g_ffn_post", "b_gates", "conv_w_mix", "conv_b_mix",
                    "conv_w_ffn", "conv_b_ffn")])
    shared = {
        "meta": f(inputs["meta_tokens"]),
        "w_in": f(inputs["w_in"][0]),
        "w_f": f(inputs["w_fourier"][0]),
        "w_co": f(inputs["w_conv_out"][0]),
        "w_out": f(inputs["w_out"][0]),
        "w_up": f(inputs["w_up"][0]),
        "w_dn": f(inputs["w_down"][0]),
        "vecs": vecs,
        "ident": ident,
        "cs128": cs128,
        "dft": dft,
    }
    x = np.asarray(inputs["x"], dtype=np.float32)
    maps = []
    for b in range(x.shape[0]):
        m = dict(shared)
        m["x"] = np.ascontiguousarray(x[b])
        maps.append(m)
    return maps


def kernel(**inputs):
    if "nc" not in _CACHE:
        _CACHE["nc"] = build_program()[0]
    nc = _CACHE["nc"]
    in_maps = make_in_maps(inputs)
    res = run_bass_kernel_spmd(nc, in_maps, core_ids=list(range(len(in_maps))))
    out = np.stack([np.asarray(r["out"], dtype=np.float32) for r in res.results], axis=0)
    return out
```

```python
import contextlib
import numpy as np
import ml_dtypes
import concourse.bass as bass
import concourse.mybir as mybir
from concourse.bass_utils import run_bass_kernel_spmd

F32 = mybir.dt.float32
BF16 = mybir.dt.bfloat16
AF = mybir.ActivationFunctionType
ALU = mybir.AluOpType

D = 1024
SEQ = 8192
NMETA = 16
L = SEQ + NMETA
LH = L // 2
NT = 33
NPAD = NT * 128
DFF = 2816
NKB = 9
EPS = 1e-6

V_G1, V_GP1, V_G2, V_GP2 = 0, 8, 16, 24
V_BG = 32
V_CWM = 48
V_CBM = 60
V_CWF = 64
V_CBF = 196
NV = 240

ENGS = ("sync", "scalar", "vector", "gpsimd", "tensor")


class Op:
    __slots__ = ("eng", "fn", "deps", "is_dma", "sig", "signaling", "dsem", "seq")

    def __init__(self, eng, fn, is_dma, dsem):
        self.eng = eng
        self.fn = fn
        self.deps = []
        self.is_dma = is_dma
        self.sig = None
        self.signaling = is_dma
        self.dsem = dsem


class Prog:
    def __init__(self):
        self.ops = {e: [] for e in ENGS}
        self.last_w = {}
        self.readers = {}
        self.same_engine_sync = {"scalar", "vector", "gpsimd"}
        self.dma_sems = {}
        self.group_sems = set()
        self.nops = 0

    def op(self, eng, fn, reads=(), writes=(), dma=False, dsem=None, group=False):
        o = Op(eng, fn, dma, dsem)
        self.nops += 1
        o.seq = self.nops
        if dma:
            assert self.dma_sems.setdefault(dsem, eng) == eng
            if group:
                self.group_sems.add(dsem)
        deps = {}
        for t in reads:
            w = self.last_w.get(t)
            if w is not None:
                deps[id(w)] = w
        for t in writes:
            w = self.last_w.get(t)
            if w is not None:
                deps[id(w)] = w
            for r in self.readers.get(t, ()):
                deps[id(r)] = r
        best = {}
        for p in deps.values():
            if p is o:
                continue
            if (not p.is_dma) and (not dma) and p.eng == eng and eng not in self.same_engine_sync:
                continue
            key = ("dma", p.dsem) if p.is_dma else ("eng", p.eng)
            q = best.get(key)
            if q is None or p.seq > q.seq:
                best[key] = p
        for p in best.values():
            o.deps.append(p)
            p.signaling = True
        for t in reads:
            self.readers.setdefault(t, []).append(o)
        for t in writes:
            self.last_w[t] = o
            self.readers[t] = []
        self.ops[eng].append(o)
        return o

    def emit(self, nc, final_wait_ops=()):
        dma_cnt = {}
        for e in ENGS:
            c = 0
            for o in self.ops[e]:
                if o.is_dma:
                    dma_cnt[o.dsem] = dma_cnt.get(o.dsem, 0) + 16
                    o.sig = (("dma", o.dsem), dma_cnt[o.dsem])
                elif o.signaling:
                    c += 1
                    o.sig = (("eng", e), c)
        for e in ENGS:
            for o in self.ops[e]:
                if o.is_dma and o.dsem in self.group_sems:
                    o.sig = (("dma", o.dsem), dma_cnt[o.dsem])
        semkeys = [("eng", e) for e in ENGS] + [("dma", d) for d in sorted(self.dma_sems)]
        with contextlib.ExitStack() as st:
            sems = {k: st.enter_context(nc.semaphore(f"s_{k[0]}_{k[1]}")) for k in semkeys}
            block = st.enter_context(nc.Block())
            stats = {}

            def make_body(e):
                def body(engobj):
                    seen = {}
                    nw = 0
                    for o in self.ops[e]:
                        for p in o.deps:
                            k, v = p.sig
                            if seen.get(k, 0) >= v:
                                continue
                            seen[k] = v
                            engobj.wait_ge(sems[k], v)
                            nw += 1
                        inst = o.fn(engobj)
                        if o.signaling:
                            inst.then_inc(sems[o.sig[0]], 16 if o.is_dma else 1)
                    if e == "sync":
                        for o in final_wait_ops:
                            k, v = o.sig
                            if seen.get(k, 0) >= v:
                                continue
                            seen[k] = v
                            engobj.wait_ge(sems[k], v)
                    stats[e] = (len(self.ops[e]), nw)
                return body

            for e in ENGS:
                getattr(block, e)(make_body(e))
            self.stats = stats


def build_program(debug=None):
    nc = bass.Bass("TRN2", target_bir_lowering=False)
    P = Prog()

    def dram_in(name, shape, dt):
        return nc.dram_tensor(name, shape, dt, kind="ExternalInput").ap()

    x_d = dram_in("x", [SEQ, D], F32)
    meta_d = dram_in("meta", [NMETA, D], F32)
    w_in_d = dram_in("w_in", [D, 4096], F32)
    w_f_d = dram_in("w_f", [512, D], F32)
    w_co_d = dram_in("w_co", [512, D], F32)
    w_out_d = dram_in("w_out", [D, D], F32)
    w_up_d = dram_in("w_up", [D, 2 * DFF], F32)
    w_dn_d = dram_in("w_dn", [DFF, D], F32)
    vecs_d = dram_in("vecs", [128, NV], F32)
    ident_d = dram_in("ident", [128, 128], F32)
    cs_d = dram_in("cs128", [128, 256], BF16)
    dft_d = dram_in("dft", [NKB * 11 * 128, 3072], BF16)
    out_d = nc.dram_tensor("out", [SEQ, D], F32, kind="ExternalOutput").ap()

    s_in = nc.dram_tensor("s_in", [D, 4096], BF16, kind="Internal").ap()
    s_f = nc.dram_tensor("s_f", [512, D], BF16, kind="Internal").ap()
    s_co = nc.dram_tensor("s_co", [512, D], BF16, kind="Internal").ap()
    s_out = nc.dram_tensor("s_out", [D, D], BF16, kind="Internal").ap()
    s_up = nc.dram_tensor("s_up", [D, 2 * DFF], BF16, kind="Internal").ap()
    s_dn = nc.dram_tensor("s_dn", [DFF, D], BF16, kind="Internal").ap()

    NSLOT = 3
    with contextlib.ExitStack() as st:
        def sb(name, shape, dt):
            return st.enter_context(nc.sbuf_tensor(name, shape, dt))

        def ps(name, shape, dt):
            return st.enter_context(nc.psum_tensor(name, shape, dt))

        VEC = sb("VEC", [128, NV], F32)
        ID = sb("ID", [128, 128], F32)
        CS = sb("CS", [128, 256], BF16)
        ONES = sb("ONES", [128, 128], BF16)
        EPSC = sb("EPSC", [128, 2], F32)
        DUM = sb("DUM", [128, 16], F32)
        WR = sb("WR", [128, NSLOT, 4096], BF16)
        R1 = sb("R1", [128, 33792], BF16)
        R2 = sb("R2", [128, 33792], BF16)
        RS = sb("RS", [128, 8, 514], F32)
        XS2 = sb("XS", [128, 2, 1024], F32)
        RS2 = sb("RS2", [128, 8, 514], F32)
        SQ = sb("SQ", [128, 2, 512], BF16)
        RSTD = sb("RSTD", [128, 512], F32)
        RT = sb("RT", [128, 512], F32)
        UCAR = sb("UCAR", [128, 44, 2], F32)
        CAR_CV = sb("CAR_CV", [128, 4, 2], F32)
        CAR_BG = sb("CAR_BG", [128, 4, 2], BF16)
        CAR_GA = sb("CAR_GA", [128, 8, 2], BF16)
        CAR_GB = sb("CAR_GB", [128, 8, 2], BF16)

        PB = [ps(f"PB{i}", [128, 512], F32) for i in range(6)]
        TRT = ps("TRT", [128, 1024], F32)
        banks = [(PB[i][:, :], f"P{i}") for i in range(6)] + [(TRT[:, 0:512], "P6"), (TRT[:, 512:1024], "P7")]

        def view(reg, byte_off, dt, shape):
            n = int(np.prod(shape[1:]))
            if dt == F32:
                v = reg[:, byte_off // 2: byte_off // 2 + 2 * n].bitcast(F32)
            else:
                v = reg[:, byte_off // 2: byte_off // 2 + n]
            if len(shape) == 3:
                v = v.rearrange("p (a b) -> p a b", a=shape[1])
            return v

        UT = view(R1, 0, BF16, [128, 4, L])
        UCS = view(R1, 0, BF16, [128, NT, 1024])
        UP = view(R2, 0, BF16, [128, 4, NPAD])
        UM = view(R2, 2 * 4 * NPAD, BF16, [128, 4, NPAD])
        FT = view(R2, 0, BF16, [128, 4, L])
        G = view(R1, 0, BF16, [128, 22, 512])
        GA = view(R1, 0, BF16, [128, 8, 514])
        GB = view(R1, 8224, BF16, [128, 8, 514])
        M = view(R1, 22528, BF16, [128, 8, 512])
        S0 = 30720
        CV = view(R1, S0, F32, [128, 4, 514])
        BG = view(R1, S0 + 8224, BF16, [128, 4, 514])
        CO = view(R1, S0 + 12336, F32, [128, 2, 512])
        Z = view(R1, S0 + 16432, BF16, [128, 4, 512])
        T1 = view(R1, S0 + 20528, F32, [128, 2, 512])
        T2 = view(R1, S0 + 24624, F32, [128, 2, 512])
        MO = view(R1, S0, F32, [128, 8, 512])
        U = view(R1, S0, F32, [128, 6, 516])
        ACC = view(R1, S0 + 12384, F32, [128, 6, 512])
        GL = view(R1, S0 + 24672, F32, [128, 2, 512])
        OSS = [(view(R1, S0 + 16384, F32, [128, 1024]), ["OS0"]), (view(R1, S0 + 20480, F32, [128, 1024]), ["OS1"])]
        SQB = view(R1, 63488, BF16, [128, 2, 512])
        RSTD2 = view(R1, 65536, F32, [128, 512])
        HTM = (M, "M")
        SQS = [SQ, SQB]
        RSTDS = [(RSTD, "RSTD"), (RSTD2, "RSTD2")]
        RT2 = view(R1, S0 + 28768, F32, [128, 512])
        RTS = [(RT, "RT"), (RT2, "RT2")]

        ASB = RS[:, 0:2, 0:512]
        xsconf = {"buf": XS2, "n": 2}
        cnt = {"bank": 0, "slot": 0, "alt": 0, "fence": 0, "xs": 0, "os": 0}

        def act(out, in_, func, reads, writes, scale=1.0, bias=None):
            kw = {} if bias is None else {"bias": bias}
            return P.op("scalar", lambda e: e.activation(out=out, in_=in_, func=func, scale=scale, **kw),
                        reads=reads, writes=writes)

        def stt(out, in0, scalar, in1, op0, op1, reads, writes):
            return P.op("vector", lambda e: e.scalar_tensor_tensor(out=out, in0=in0, scalar=scalar, in1=in1,
                                                                   op0=op0, op1=op1), reads=reads, writes=writes)

        def tt(eng, out, in0, in1, op, reads, writes):
            return P.op(eng, lambda e: e.tensor_tensor(out=out, in0=in0, in1=in1, op=op), reads=reads, writes=writes)

        def cp(eng, out, in_, reads, writes):
            if eng == "scalar":
                return P.op(eng, lambda e: e.copy(out=out, in_=in_), reads=reads, writes=writes)
            return P.op(eng, lambda e: e.tensor_copy(out=out, in_=in_), reads=reads, writes=writes)

        def mset(eng, ap, val, writes, reads=()):
            return P.op(eng, lambda e: e.memset(ap, val), reads=reads, writes=writes)

        def mm(out, lhsT, rhs, start, stop, reads, writes):
            return P.op("tensor", lambda e: e.matmul(out, lhsT=lhsT, rhs=rhs, start=start, stop=stop),
                        reads=reads, writes=writes)

        def tr(out, in_, ident, reads, writes):
            return P.op("tensor", lambda e: e.transpose(out=out, in_=in_, identity=ident), reads=reads, writes=writes)

        def dma(q, out, in_, dsem, reads, writes, group=False):
            return P.op(q, lambda e: e.dma_start(out=out, in_=in_), reads=reads, writes=writes, dma=True,
                        dsem=dsem, group=group)

        def fence(reads):
            cnt["fence"] += 1
            tok = f"fence{cnt['fence']}"
            k = cnt["fence"] % 16
            mset("gpsimd", DUM[:, k:k + 1], 0.0, writes=[tok, f"DUM{k}"] + list(reads))
            return tok

        def next_bank():
            b = cnt["bank"] % 4
            cnt["bank"] += 1
            return banks[b]

        SSBS = [banks[4], banks[5]]
        TRA = (TRT[:, :], ["P6", "P7"])

        def alt_eng():
            cnt["alt"] += 1
            return "scalar" if cnt["alt"] % 2 else "vector"

        dbg_outs = {}

        def dbg(name, ap, shape, dt, reads):
            if debug and name in debug:
                t = nc.dram_tensor("dbg_" + name, shape, dt, kind="ExternalOutput").ap()
                o = dma("sync", t, ap, "dbg_" + name, reads=reads, writes=[])
                dbg_outs[name] = o

        def load_unit(src, kc, ranges, src_tok):
            slot = cnt["slot"] % NSLOT
            cnt["slot"] += 1
            tot = sum(n for _, n in ranges)
            tok = f"W{slot}"
            v = WR[:, slot, 0:kc * tot].rearrange("p (k n) -> p k n", k=kc)
            if len(ranges) == 1:
                c0, n = ranges[0]
                s3 = src.rearrange("(k p) n -> p k n", p=128)
                dma("sync", v, s3[:, :, c0:c0 + n], f"wr{slot}", reads=[src_tok], writes=[tok])
            else:
                (c0, n), (c1, n1) = ranges
                assert n == n1
                s4 = src.rearrange("(k p) (h n) -> p k h n", p=128, h=2)
                assert c1 - c0 == s4.shape[3]
                v4 = WR[:, slot, 0:kc * tot].rearrange("p (k h n) -> p k h n", k=kc, h=2)
                dma("sync", v4, s4[:, :, :, c0:c0 + n], f"wr{slot}", reads=[src_tok], writes=[tok])
            return v, [tok]

        dma("sync", VEC[:, :], vecs_d[:, :], "c0", reads=[], writes=["VEC"])
        dma("sync", ID[:, :], ident_d[:, :], "c1", reads=[], writes=["ID"])
        dma("sync", CS[:, :], cs_d[:, :], "c2", reads=[], writes=["CS"])
        mset("gpsimd", ONES[:, :], 1.0, writes=["ONES"])
        mset("gpsimd", EPSC[:, :], EPS, writes=["EPSC"])
        mset("gpsimd", UCAR[:, :, :], 0.0, writes=[f"UCAR{c}" for c in range(44)])

        def cast_weight(src, dst, rows, cols, cpiece, name):
            for r0 in range(0, rows, 128):
                for c0 in range(0, cols, cpiece):
                    dma("gpsimd", dst[r0:r0 + 128, c0:c0 + cpiece], src[r0:r0 + 128, c0:c0 + cpiece],
                        "cast_" + name, reads=[], writes=["SCR_" + name + f"_{r0}_{c0}"], group=True)
            return "SCR_" + name + f"_{rows - 128}_{cols - cpiece}"

        T_IN0 = cast_weight(w_in_d[:, 0:512], s_in[:, 0:512], D, 512, 512, "in0")
        T_IN = cast_weight(w_in_d[:, 512:4096], s_in[:, 512:4096], D, 3584, 1792, "in")
        T_F = cast_weight(w_f_d, s_f, 512, D, 1024, "f")
        T_CO = cast_weight(w_co_d, s_co, 512, D, 1024, "co")
        T_OUT = cast_weight(w_out_d, s_out, D, D, 1024, "out")
        s_up_r = s_up.rearrange("r (u h j) -> r u h j", h=2, j=256)
        for r0 in range(0, D, 128):
            for h in range(2):
                dma("gpsimd", s_up_r[r0:r0 + 128, :, h, :],
                    w_up_d[r0:r0 + 128, DFF * h:DFF * (h + 1)].rearrange("r (u j) -> r u j", j=256),
                    "cast_up", reads=[], writes=[f"SCR_up_{r0}_{h}"], group=True)
        T_UP = f"SCR_up_{D - 128}_1"
        T_DN = cast_weight(w_dn_d, s_dn, DFF, D, 1024, "dn")

        def load_x_sub(col0, sub, r, ta, rsb):
            rs, rsn = rsb
            XS, nxs = xsconf["buf"], xsconf["n"]
            slot = cnt["xs"] % nxs
            cnt["xs"] += 1
            tok = f"XS{slot}"
            if ta < NMETA:
                nm = min(NMETA - ta, r)
                dma("sync", XS[0:nm, slot, :], meta_d[ta:ta + nm, :], f"xs{slot}", reads=[], writes=[tok])
                if r > nm:
                    dma("sync", XS[nm:r, slot, :], x_d[0:r - nm, :], f"xs{slot}", reads=[], writes=[tok])
            else:
                dma("sync", XS[0:r, slot, :], x_d[ta - NMETA:ta - NMETA + r, :], f"xs{slot}", reads=[], writes=[tok])
            for dc in range(8):
                tr(TRT[:, 128 * dc:128 * dc + r], XS[0:r, slot, 128 * dc:128 * dc + 128], ID[0:r, 0:r],
                   reads=[tok, "ID"], writes=TRA[1])
            src = TRT[:, :].rearrange("p (c n) -> p c n", c=8)[:, :, 0:r]
            cp("scalar", rs[:, :, col0 + 128 * sub:col0 + 128 * sub + r], src, reads=[],
               writes=TRA[1] + [f"{rsn}{dc}" for dc in range(8)])

        def load_x_T(col0, n, t0, rsb):
            for sub in range((n + 127) // 128):
                load_x_sub(col0, sub, min(128, n - 128 * sub), t0 + 128 * sub, rsb)

        def finish_rstd(n, which=0):
            ssb = SSBS[which]
            rstd, rtok = RSTDS[which]
            rt, rttok = RTS[which]
            act(rt[:, 0:n], ssb[0][:, 0:n], AF.Sqrt, reads=["EPSC"], writes=[ssb[1], rttok], scale=1.0 / D,
                bias=EPSC[:, 0:1])
            P.op("vector", lambda e: e.reciprocal(out=rstd[:, 0:n], in_=rt[:, 0:n]), reads=[rttok], writes=[rtok])

        def norm_part1(colA, n, rsb, which=0, extra=()):
            rs, rsn = rsb
            ssb = SSBS[which]
            for dc in range(8):
                sq = SQS[which][:, dc % 2, 0:n]
                sqt = f"SQ{which}{dc % 2}"
                act(sq, rs[:, dc, colA:colA + n], AF.Square, reads=[f"{rsn}{dc}"] + list(extra), writes=[sqt])
                mm(ssb[0][:, 0:n], ONES[:, :], sq, dc == 0, dc == 7, reads=[sqt, "ONES"], writes=[ssb[1]])

        def norm_part2(colA, n, gcol, rsb, htb, which=0, extra=(), do_finish=True):
            rs, rsn = rsb
            ht, htn = htb
            rstd, rtok = RSTDS[which]
            if do_finish:
                finish_rstd(n, which)
            for dc in range(8):
                stt(ht[:, dc, 0:n], rs[:, dc, colA:colA + n], VEC[:, gcol + dc:gcol + dc + 1], rstd[:, 0:n],
                    ALU.mult, ALU.mult, reads=[f"{rsn}{dc}", "VEC", rtok] + list(extra), writes=[f"{htn}{dc}"])

        def norm_to_HT(colA, n, gcol, rsb, htb, which=0, extra=()):
            norm_part1(colA, n, rsb, which, extra)
            norm_part2(colA, n, gcol, rsb, htb, which, extra)

        HT_ALL = [f"HT{dc}" for dc in range(8)]

        WU, WU_T = load_unit(s_in, 8, [(0, 512)], T_IN0)
        HTA = view(R2, 0, BF16, [128, 8, 512])
        HTB = view(R2, 8192, BF16, [128, 8, 512])
        p1buf = [((RS, "RS"), (HTA, "HTa")), ((RS2, "RQ"), (HTB, "HTb"))]
        XS8 = view(R2, 16384, F32, [128, 8, 1024])
        xsconf["buf"], xsconf["n"] = XS8, 8

        def p1_width(tb):
            return 512 if tb < 16 else 16

        sv = (SQS[1], RSTDS[1], RTS[1])
        SQS[1] = view(R2, 49152, BF16, [128, 2, 512])
        RSTDS[1] = (view(R2, 51200, F32, [128, 512]), "RSTD2")
        RTS[1] = (view(R2, 53248, F32, [128, 512]), "RT2")
        load_x_T(2, p1_width(0), 0, rsb=p1buf[0][0])
        norm_part1(2, p1_width(0), p1buf[0][0], 0)
        finish_rstd(p1_width(0), 0)
        for tb in range(17):
            n = p1_width(tb)
            t0 = 512 * tb
            rsb, htb = p1buf[tb % 2]
            if tb + 1 < 17:
                load_x_T(2, p1_width(tb + 1), 512 * (tb + 1), rsb=p1buf[(tb + 1) % 2][0])
            norm_part2(2, n, V_G1, rsb, htb, tb % 2, do_finish=False)
            for g in range(4):
                bk, btok = next_bank()
                for dc in range(8):
                    mm(bk[:, 0:n], WU[:, dc, 128 * g:128 * g + 128], htb[0][:, dc, 0:n], dc == 0, dc == 7,
                       reads=WU_T + [f"{htb[1]}{dc}"], writes=[btok])
                cp(alt_eng(), UT[:, g, t0:t0 + n], bk[:, 0:n], reads=[], writes=[btok, f"UT{g}"])
            if tb + 1 < 17:
                norm_part1(2, p1_width(tb + 1), p1buf[(tb + 1) % 2][0], (tb + 1) % 2)
                finish_rstd(p1_width(tb + 1), (tb + 1) % 2)
        f_p1 = fence([f"HTa{dc}" for dc in range(8)] + [f"HTb{dc}" for dc in range(8)] + [f"XS{i}" for i in range(8)]
                     + ["SQ10", "SQ11", "RSTD2", "RT2"])
        SQS[1], RSTDS[1], RTS[1] = sv
        xsconf["buf"], xsconf["n"] = XS2, 2
        cnt["xs"] = 0
        dbg("ut", UT[:, :, :], [128, 4, L], BF16, reads=[f"UT{g}" for g in range(4)])

        for g in range(4):
            mset("gpsimd", UP[:, g, LH + 1:NPAD], 0.0, writes=[f"UP{g}"], reads=[f_p1])
            mset("gpsimd", UM[:, g, LH:NPAD], 0.0, writes=[f"UM{g}"], reads=[f_p1])
            mset("gpsimd", UM[:, g, 0:1], 0.0, writes=[f"UM{g}"])
            a = UT[:, g, 1:LH]
            b = UT[:, g, L - 1:LH:-1]
            tt("vector", UP[:, g, 1:LH], a, b, ALU.add, reads=[f"UT{g}"], writes=[f"UP{g}"])
            tt("vector", UM[:, g, 1:LH], a, b, ALU.subtract, reads=[f"UT{g}"], writes=[f"UM{g}"])
            cp("gpsimd", UP[:, g, 0:1], UT[:, g, 0:1], reads=[f"UT{g}"], writes=[f"UP{g}"])
            cp("gpsimd", UP[:, g, LH:LH + 1], UT[:, g, LH:LH + 1], reads=[f"UT{g}"], writes=[f"UP{g}"])
        f_r1 = fence([f"UT{g}" for g in range(4)])
        for i in range(NT):
            for g in range(4):
                mm(TRT[:, 256 * g:256 * g + 128], UP[:, g, 128 * i:128 * i + 128], CS[:, 0:128], True, True,
                   reads=[f"UP{g}", "CS"], writes=TRA[1])
                mm(TRT[:, 256 * g + 128:256 * g + 256], UM[:, g, 128 * i:128 * i + 128], CS[:, 128:256], True, True,
                   reads=[f"UM{g}", "CS"], writes=TRA[1])
            cp(alt_eng(), UCS[:, i, :], TRT[:, :], reads=[f_r1], writes=TRA[1] + [f"UCS{i}"])
        dbg("ucs", UCS[:, :, :], [128, NT, 1024], BF16, reads=[f"UCS{i}" for i in range(NT)])
        f_r2 = fence([f"UP{g}" for g in range(4)] + [f"UM{g}" for g in range(4)])

        it = 0
        for gp in range(2):
            for kb in range(NKB):
                wk = 512 if kb < NKB - 1 else LH + 1 - 512 * (NKB - 1)
                k0 = 512 * kb
                bset = [banks[(it % 2) * 4 + q] for q in range(4)]
                it += 1
                for t in range(11):
                    slot = cnt["slot"] % NSLOT
                    cnt["slot"] += 1
                    row0 = (kb * 11 + t) * 128
                    dv = WR[:, slot, 0:3072]
                    dtoks = [f"W{slot}"]
                    dma("sync", dv, dft_d[row0:row0 + 128, :], f"wr{slot}", reads=[], writes=dtoks)
                    dv4 = dv.rearrange("p (j c k) -> p j c k", j=3, c=2)
                    for j in range(3):
                        i = 3 * t + j
                        for gi in range(2):
                            g = 2 * gp + gi
                            mm(bset[2 * gi][0][:, 0:wk], UCS[:, i, 256 * g:256 * g + 128], dv4[:, j, 0, 0:wk],
                               i == 0, i == NT - 1, reads=dtoks + [f"UCS{i}"], writes=[bset[2 * gi][1]])
                            mm(bset[2 * gi + 1][0][:, 0:wk], UCS[:, i, 256 * g + 128:256 * g + 256],
                               dv4[:, j, 1, 0:wk], i == 0, i == NT - 1, reads=dtoks + [f"UCS{i}"],
                               writes=[bset[2 * gi + 1][1]])
                for gi in range(2):
                    g = 2 * gp + gi
                    a_ps, a_tok = bset[2 * gi]
                    b_ps, b_tok = bset[2 * gi + 1]
                    asb = ASB[:, gi, 0:wk]
                    cp("scalar", asb, a_ps[:, 0:wk], reads=[], writes=[a_tok, f"RS{gi}"])
                    tt("vector", FT[:, g, k0:k0 + wk], asb, b_ps[:, 0:wk], ALU.add, reads=[f"RS{gi}", f_r2],
                       writes=[b_tok, f"FT{g}"])
                    lo = 1 if kb == 0 else 0
                    hi = wk if kb < NKB - 1 else wk - 1
                    m = hi - lo
                    dst = FT[:, g, L - (k0 + lo):L - (k0 + lo) - m:-1]
                    tt("vector", dst, asb[:, lo:hi], b_ps[:, lo:hi], ALU.subtract, reads=[f"RS{gi}", f_r2],
                       writes=[b_tok, f"FT{g}"])
        dbg("ft", FT[:, :, :], [128, 4, L], BF16, reads=[f"FT{g}" for g in range(4)])
        f_p2 = fence([f"UCS{i}" for i in range(NT)])
        FT_ALL = [f"FT{g}" for g in range(4)]

        rsbufs = [(RS, "RS"), (RS2, "RQ")]

        def rs_of(j):
            return rsbufs[j % 2]

        CVT = [f"CV{q}" for q in range(4)]
        BGT = [f"BG{q}" for q in range(4)]
        GAT = [f"GA{q}" for q in range(8)]
        GBT = [f"GB{q}" for q in range(8)]
        MOT = [f"MO{q}" for q in range(8)]
        GT = [f"G{c}" for c in range(22)]
        U_VCB = (2, 0, 1)
        U_GATES = (3, 4, 5, 6)

        def A_proj(n, units, fA, after_unit=None):
            HT = M
            for u in units:
                wv, wtok = load_unit(s_in, 8, [(512 + 512 * u, 512)], T_IN)
                for q in range(4):
                    bk, btok = next_bank()
                    for dc in range(8):
                        mm(bk[:, 0:n], wv[:, dc, 128 * q:128 * q + 128], HT[:, dc, 0:n], dc == 0, dc == 7,
                           reads=wtok + [f"M{dc}"], writes=[btok])
                    if u == 2:
                        cp("scalar", CV[:, q, 2:2 + n], bk[:, 0:n], reads=[fA], writes=[btok, f"CV{q}"])
                    elif u == 0:
                        tt("vector", CV[:, q, 2:2 + n], bk[:, 0:n], CV[:, q, 2:2 + n], ALU.mult, reads=[fA],
                           writes=[btok, f"CV{q}"])
                    elif u == 1:
                        cp("scalar", BG[:, q, 2:2 + n], bk[:, 0:n], reads=[fA], writes=[btok, f"BG{q}"])
                    else:
                        gi = (u - 3) * 4 + q
                        dst = GA if gi < 8 else GB
                        nm = "GA" if gi < 8 else "GB"
                        act(dst[:, gi % 8, 2:2 + n], bk[:, 0:n], AF.Sigmoid, reads=[fA, "VEC"],
                            writes=[btok, f"{nm}{gi % 8}"], bias=VEC[:, V_BG + gi:V_BG + gi + 1])
                if after_unit and u in after_unit:
                    after_unit[u]()

        def save_carries(n):
            cp("gpsimd", CAR_CV[:, :, :], CV[:, :, n:n + 2], reads=CVT, writes=["CAR_CV"])
            cp("gpsimd", CAR_BG[:, :, :], BG[:, :, n:n + 2], reads=BGT, writes=["CAR_BG"])
            cp("gpsimd", CAR_GA[:, :, :], GA[:, :, n:n + 2], reads=GAT, writes=["CAR_GA"])
            cp("gpsimd", CAR_GB[:, :, :], GB[:, :, n:n + 2], reads=GBT, writes=["CAR_GB"])

        def restore_gates(fA):
            cp("gpsimd", GA[:, :, 0:2], CAR_GA[:, :, :], reads=["CAR_GA", fA], writes=GAT)
            cp("gpsimd", GB[:, :, 0:2], CAR_GB[:, :, :], reads=["CAR_GB", fA], writes=GBT)

        def restore_vcb(fA):
            cp("gpsimd", CV[:, :, 0:2], CAR_CV[:, :, :], reads=["CAR_CV", fA], writes=CVT)
            cp("gpsimd", BG[:, :, 0:2], CAR_BG[:, :, :], reads=["CAR_BG", fA], writes=BGT)

        def pn_matmuls(n, produce, f_mo, which=0):
            ssb = SSBS[which]
            cntd = 0
            for dco, bk, btok in produce():
                cp("scalar", MO[:, dco, 0:n], bk[:, 0:n], reads=[f_mo], writes=[btok, f"MO{dco}"])
                sq = SQS[which][:, cntd % 2, 0:n]
                sqt = f"SQ{which}{cntd % 2}"
                act(sq, bk[:, 0:n], AF.Square, reads=[], writes=[btok, sqt])
                mm(ssb[0][:, 0:n], ONES[:, :], sq, cntd == 0, cntd == 7, reads=[sqt, "ONES"], writes=[ssb[1]])
                cntd += 1

        def pn_finish(n, gcol, rsb, colR, add_into_rs, which=0):
            rs, rsn = rsb
            rstd, rtok = RSTDS[which]
            finish_rstd(n, which)
            for dc in range(8):
                stt(MO[:, dc, 0:n], MO[:, dc, 0:n], VEC[:, gcol + dc:gcol + dc + 1], rstd[:, 0:n], ALU.mult, ALU.mult,
                    reads=["VEC", rtok], writes=[f"MO{dc}"])
                if add_into_rs:
                    tt("gpsimd", rs[:, dc, colR:colR + n], rs[:, dc, colR:colR + n], MO[:, dc, 0:n], ALU.add,
                       reads=[f"MO{dc}"], writes=[f"{rsn}{dc}"])
                else:
                    tt("gpsimd", MO[:, dc, 0:n], MO[:, dc, 0:n], rs[:, dc, colR:colR + n], ALU.add,
                       reads=[f"{rsn}{dc}"], writes=[f"MO{dc}"])

        CO4 = view(R1, S0 + 20528, F32, [128, 4, 512])
        CO4T = ["T10", "T11", "T20", "T21"]

        def conv_co(nB, fS):
            cwm = lambda k, q: VEC[:, V_CWM + 4 * k + q:V_CWM + 4 * k + q + 1]
            for q in range(4):
                co = CO4[:, q, 0:nB]
                ctok = CO4T[q]
                act(co, CV[:, q, 1:1 + nB], AF.Identity, reads=[f"CV{q}", "VEC", fS], writes=[ctok], scale=cwm(1, q),
                    bias=VEC[:, V_CBM + q:V_CBM + q + 1])
                stt(co, CV[:, q, 0:nB], cwm(0, q), co, ALU.mult, ALU.add, reads=[f"CV{q}", "VEC"], writes=[ctok])
                stt(co, CV[:, q, 2:2 + nB], cwm(2, q), co, ALU.mult, ALU.add, reads=[f"CV{q}", "VEC"], writes=[ctok])

        def stage_B(nB, s, rsb, fS, conv_done=False):
            if not conv_done:
                conv_co(nB, fS)
            for q in range(4):
                tt("gpsimd", Z[:, q, 0:nB], BG[:, q, 1:1 + nB], CO4[:, q, 0:nB], ALU.mult,
                   reads=[f"BG{q}", CO4T[q], fS], writes=[f"Z{q}"])
            tb0 = s - 1
            for h in range(2):
                wco, wco_t = load_unit(s_co, 4, [(512 * h, 512)], T_CO)
                wf, wf_t = load_unit(s_f, 4, [(512 * h, 512)], T_F)
                for dq in range(4):
                    dc = 4 * h + dq
                    bkb, btb = next_bank()
                    for q in range(4):
                        mm(bkb[:, 0:nB], wco[:, q, 128 * dq:128 * dq + 128], Z[:, q, 0:nB], q == 0, q == 3,
                           reads=wco_t + [f"Z{q}"], writes=[btb])
                    bka, bta = next_bank()
                    for g in range(4):
                        mm(bka[:, 0:nB], wf[:, g, 128 * dq:128 * dq + 128], FT[:, g, tb0:tb0 + nB], g == 0, g == 3,
                           reads=wf_t + [f"FT{g}"], writes=[bta])
                    t1 = T1[:, dc % 2, 0:nB]
                    t2 = T2[:, dc % 2, 0:nB]
                    tt("vector", t1, bka[:, 0:nB], GA[:, dc, 1:1 + nB], ALU.mult, reads=[f"GA{dc}", fS],
                       writes=[bta, f"T1{dc % 2}"])
                    tt("vector", t2, bkb[:, 0:nB], GB[:, dc, 1:1 + nB], ALU.mult, reads=[f"GB{dc}", fS],
                       writes=[btb, f"T2{dc % 2}"])
                    tt("gpsimd", M[:, dc, 0:nB], t1, t2, ALU.add, reads=[f"T1{dc % 2}", f"T2{dc % 2}"],
                       writes=[f"M{dc}"])
            f1 = fence(CVT + BGT + GAT + GBT + [f"Z{q}" for q in range(4)] + ["CO0", "CO1", "T10", "T11", "T20", "T21"])

            def produce_out():
                for h in range(2):
                    wv, wtok = load_unit(s_out, 8, [(512 * h, 512)], T_OUT)
                    for dq in range(4):
                        bk, btok = next_bank()
                        for k in range(8):
                            mm(bk[:, 0:nB], wv[:, k, 128 * dq:128 * dq + 128], M[:, k, 0:nB], k == 0, k == 7,
                               reads=wtok + [f"M{k}"], writes=[btok])
                        yield 4 * h + dq, bk, btok
            pn_matmuls(nB, produce_out, f1, 0)
            pn_finish(nB, V_GP1, rsb, 1, True, 0)
            f2 = fence(MOT)
            return f1, f2

        def stage_CD(nB, nD, f1, f2, epilogue, rsb, hooks):
            HT = M
            norm_to_HT(1, nB, V_G2, rsb, HTM, 0)
            cwf = lambda k, c: VEC[:, V_CWF + 44 * k + c:V_CWF + 44 * k + c + 1]

            def conv_center(c):
                for half in range(2):
                    ch = c + 22 * half
                    ui = (2 * c + half) % 6
                    ut = [f"U{ui}c", f"U{ui}d"]
                    act(ACC[:, ui, 0:nD], U[:, ui, 1:1 + nD], AF.Identity, reads=ut + ["VEC", f2], writes=[f"ACC{ui}"],
                        scale=cwf(1, ch), bias=VEC[:, V_CBF + ch:V_CBF + ch + 1])

            def conv_side(c):
                for half in range(2):
                    ch = c + 22 * half
                    ui = (2 * c + half) % 6
                    ut = [f"U{ui}c", f"U{ui}d"]
                    atok = f"ACC{ui}"
                    acc = ACC[:, ui, 0:nD]
                    stt(acc, U[:, ui, 0:nD], cwf(0, ch), acc, ALU.mult, ALU.add, reads=ut + ["VEC"], writes=[atok])
                    stt(acc, U[:, ui, 2:2 + nD], cwf(2, ch), acc, ALU.mult, ALU.add, reads=ut + ["VEC"], writes=[atok])

            def gelu_mul(c):
                a0 = (2 * c) % 6
                a1 = (2 * c + 1) % 6
                gl = GL[:, c % 2, 0:nD]
                gtok = f"GL{c % 2}"
                act(gl, ACC[:, a0, 0:nD], AF.Gelu_apprx_tanh, reads=[f"ACC{a0}", f2], writes=[gtok])
                tt("gpsimd", G[:, c, 0:nD], gl, ACC[:, a1, 0:nD], ALU.mult, reads=[gtok, f"ACC{a1}", f1],
                   writes=[f"G{c}"])

            def mm_pair(c, wv, wtok, p):
                res = []
                for half in range(2):
                    bk, btok = next_bank()
                    lo = 256 * half + 128 * p
                    for dc in range(8):
                        mm(bk[:, 0:nB], wv[:, dc, lo:lo + 128], HT[:, dc, 0:nB], dc == 0, dc == 7,
                           reads=wtok + [f"M{dc}"], writes=[btok])
                    res.append((bk, btok))
                return res

            def evac_pair(c, res):
                for half in range(2):
                    ch = c + 22 * half
                    bk, btok = res[half]
                    ui = (2 * c + half) % 6
                    cp("gpsimd", U[:, ui, 0:2], UCAR[:, ch, :], reads=[f"UCAR{ch}", f2], writes=[f"U{ui}c"])
                    if epilogue:
                        mset("gpsimd", U[:, ui, 2 + nB:4 + nB], 0.0, writes=[f"U{ui}d"], reads=[f2])
                    cp("scalar", U[:, ui, 2:2 + nB], bk[:, 0:nB], reads=[f2], writes=[btok, f"U{ui}d"])
                    if not epilogue:
                        cp("gpsimd", UCAR[:, ch, :], U[:, ui, nB:nB + 2], reads=[f"U{ui}d"], writes=[f"UCAR{ch}"])

            for u in range(11):
                wv, wtok = load_unit(s_up, 8, [(512 * u, 512)], T_UP)
                for p in range(2):
                    c = 2 * u + p
                    res = mm_pair(c, wv, wtok, p)
                    if c in hooks:
                        hooks[c]()
                    if c >= 1:
                        conv_center(c - 1)
                        conv_side(c - 1)
                    evac_pair(c, res)
                    if c >= 2:
                        gelu_mul(c - 2)
            conv_center(21)
            conv_side(21)
            gelu_mul(20)
            gelu_mul(21)
            f3 = fence([f"U{i}{x}" for i in range(6) for x in "cd"] + [f"ACC{i}" for i in range(6)] + ["GL0", "GL1"])
            return f3

        def D2a(nD, f3):
            def produce_dn():
                for pq in range(4):
                    bks = [next_bank(), next_bank()]
                    for kh in range(2):
                        wv, wtok = load_unit(s_dn[1408 * kh:1408 * (kh + 1), :], 11, [(256 * pq, 256)], T_DN)
                        for d2 in range(2):
                            for k in range(11):
                                kk = 11 * kh + k
                                mm(bks[d2][0][:, 0:nD], wv[:, k, 128 * d2:128 * d2 + 128], G[:, kk, 0:nD],
                                   kh == 0 and k == 0, kh == 1 and k == 10, reads=wtok + [f"G{kk}"],
                                   writes=[bks[d2][1]])
                    for d2 in range(2):
                        yield 2 * pq + d2, bks[d2][0], bks[d2][1]
            pn_matmuls(nD, produce_dn, f3, 0)

        def out_stage(nD, s, f3):
            tD0 = s - 2
            k0 = max(0, NMETA - tD0)
            while k0 < nD:
                r = min(128, nD - k0)
                for dc in range(8):
                    tr(TRT[0:r, 128 * dc:128 * dc + 128], MO[:, dc, k0:k0 + r], ID[:, :], reads=[f"MO{dc}", "ID"],
                       writes=TRA[1])
                oi = cnt["os"] % 2
                osb, ostoks = OSS[oi]
                cnt["os"] += 1
                cp("scalar", osb[0:r, :], TRT[0:r, :], reads=[f3], writes=TRA[1] + ostoks)
                row0 = tD0 + k0 - NMETA
                o = dma("gpsimd", out_d[row0:row0 + r, :], osb[0:r, :], f"outd{oi}", reads=ostoks, writes=[])
                out_ops.append(o)
                k0 += r

        out_ops = []

        def carry_rs(src, dst, n):
            cp("gpsimd", dst[0][:, :, 0:2], src[0][:, :, n:n + 2], reads=[f"{src[1]}{q}" for q in range(8)],
               writes=[f"{dst[1]}{q}" for q in range(8)])

        rp = rs_of(-1)
        load_x_T(2, 2, 14, rp)
        norm_to_HT(2, 2, V_G1, rp, HTM, 0, extra=[f_p2])
        A_proj(2, U_VCB + U_GATES, f_p2)
        save_carries(2)
        carry_rs(rp, rs_of(0), 2)
        load_x_T(2, 512, NMETA, rs_of(0))
        norm_to_HT(2, 512, V_G1, rs_of(0), HTM, 0)
        restore_gates(f_p2)
        restore_vcb(f_p2)
        A_proj(512, U_VCB + U_GATES, f_p2)
        save_carries(512)
        fS = f_p2
        for j in range(16):
            s = NMETA + 512 * j
            rsb = rs_of(j)
            nxt = j < 15
            f1, f2 = stage_B(512, s, rsb, fS, conv_done=(j > 0))
            hooks = {}
            if nxt:
                for k in range(4):
                    hooks[3 + 4 * k] = (lambda k=k, j=j: load_x_sub(2, k, 128, NMETA + 512 * (j + 1) + 128 * k,
                                                                    rs_of(j + 1)))
                hooks[19] = (lambda j=j: norm_part1(2, 512, rs_of(j + 1), 1))
            f3 = stage_CD(512, 512, f1, f2, False, rsb, hooks)
            carry_rs(rsb, rs_of(j + 1), 512)
            if nxt:
                norm_part2(2, 512, V_G1, rs_of(j + 1), HTM, 1)
            D2a(512, f3)
            fG = fence(GT)
            if nxt:
                restore_gates(fG)
            pn_finish(512, V_GP2, rsb, 0, False, 0)
            if nxt:
                A_proj(512, U_GATES, fG)
            out_stage(512, s, f3)
            fS = fence(MOT + ["OS0", "OS1"])
            if nxt:
                restore_vcb(fS)
                A_proj(512, U_VCB, fS, after_unit={0: (lambda fS=fS: conv_co(512, fS))})
                save_carries(512)
            if j == 0:
                dbg("rs0", RS[:, :, :], [128, 8, 514], F32, reads=[f"RS{q}" for q in range(8)])
        s = L
        rsb = rs_of(16)
        restore_gates(fS)
        restore_vcb(fS)
        for q in range(4):
            mset("gpsimd", CV[:, q, 2:3], 0.0, writes=[f"CV{q}"])
        f1, f2 = stage_B(1, s, rsb, fS)
        f3 = stage_CD(1, 2, f1, f2, True, rsb, {})
        D2a(2, f3)
        pn_finish(2, V_GP2, rsb, 0, False, 0)
        out_stage(2, s, f3)

        finals = list(out_ops[-2:]) + list(dbg_outs.values())
        P.emit(nc, final_wait_ops=finals)
    return nc, P


_CACHE = {}


def _consts():
    if "c" in _CACHE:
        return _CACHE["c"]
    bf = ml_dtypes.bfloat16
    c = np.arange(128)
    ang = 2 * np.pi * np.outer(c, c) / 128.0
    cs = np.concatenate([np.cos(ang), np.sin(ang)], axis=1) / np.sqrt(128.0)
    cs128 = cs.astype(np.float32).astype(bf)
    ident = np.eye(128, dtype=np.float32)
    n = np.arange(NPAD, dtype=np.int64)[:, None]
    k = np.arange(NKB * 512, dtype=np.int64)[None, :]
    m = (n * k) % L
    angp = (2 * np.pi / L) * m.astype(np.float64)
    valid = ((n <= LH) & (k <= LH))
    sc = 1.0 / np.sqrt(float(L))
    cm = np.where(valid, np.cos(angp) * sc, 0.0).astype(np.float32).astype(bf)
    sm = np.where(valid, -np.sin(angp) * sc, 0.0).astype(np.float32).astype(bf)
    del angp, m
    cm6 = cm.reshape(11, 3, 128, NKB, 512)
    sm6 = sm.reshape(11, 3, 128, NKB, 512)
    both = np.stack([cm6, sm6], axis=0)
    dft = np.ascontiguousarray(both.transpose(4, 1, 3, 2, 0, 5)).reshape(NKB * 11 * 128, 3072)
    _CACHE["c"] = (cs128, ident, dft)
    return _CACHE["c"]


def _vecs(g_mix_pre, g_mix_post, g_ffn_pre, g_ffn_post, b_gates, conv_w_mix, conv_b_mix, conv_w_ffn, conv_b_ffn):
    v = np.zeros((128, NV), np.float32)

    def colmajor(a):
        return np.asarray(a, np.float32).reshape(-1, 128).T

    v[:, V_G1:V_G1 + 8] = colmajor(g_mix_pre[0])
    v[:, V_GP1:V_GP1 + 8] = colmajor(g_mix_post[0])
    v[:, V_G2:V_G2 + 8] = colmajor(g_ffn_pre[0])
    v[:, V_GP2:V_GP2 + 8] = colmajor(g_ffn_post[0])
    v[:, V_BG:V_BG + 16] = colmajor(b_gates[0])
    for k in range(3):
        v[:, V_CWM + 4 * k:V_CWM + 4 * k + 4] = colmajor(conv_w_mix[0, k])
        v[:, V_CWF + 44 * k:V_CWF + 44 * k + 44] = colmajor(conv_w_ffn[0, k])
    v[:, V_CBM:V_CBM + 4] = colmajor(conv_b_mix[0])
    v[:, V_CBF:V_CBF + 44] = colmajor(conv_b_ffn[0])
    return v


def make_in_maps(inputs):
    cs128, ident, dft = _consts()
    f = lambda a: np.ascontiguousarray(np.asarray(a, dtype=np.float32))
    vecs = _vecs(*[np.asarray(inputs[k], np.float32) for k in
                   ("g_mix_pre", "g_mix_post", "g_ffn_pre", "g_ffn_post", "b_gates", "conv_w_mix", "conv_b_mix",
                    "conv_w_ffn", "conv_b_ffn")])
    shared = {
        "meta": f(inputs["meta_tokens"]),
        "w_in": f(inputs["w_in"][0]),
        "w_f": f(inputs["w_fourier"][0]),
        "w_co": f(inputs["w_conv_out"][0]),
        "w_out": f(inputs["w_out"][0]),
        "w_up": f(inputs["w_up"][0]),
        "w_dn": f(inputs["w_down"][0]),
        "vecs": vecs,
        "ident": ident,
        "cs128": cs128,
        "dft": dft,
    }
    x = np.asarray(inputs["x"], dtype=np.float32)
    maps = []
    for b in range(x.shape[0]):
        m = dict(shared)
        m["x"] = np.ascontiguousarray(x[b])
        maps.append(m)
    return maps


def kernel(**inputs):
    if "nc" not in _CACHE:
        _CACHE["nc"] = build_program()[0]
    nc = _CACHE["nc"]
    in_maps = make_in_maps(inputs)
    res = run_bass_kernel_spmd(nc, in_maps, core_ids=list(range(len(in_maps))))
    out = np.stack([np.asarray(r["out"], dtype=np.float32) for r in res.results], axis=0)
    return out
```

```python
import contextlib
import numpy as np
import ml_dtypes
import concourse.bass as bass
import concourse.mybir as mybir
from concourse.bass_utils import run_bass_kernel_spmd

F32 = mybir.dt.float32
BF16 = mybir.dt.bfloat16
AF = mybir.ActivationFunctionType
ALU = mybir.AluOpType

D = 1024
SEQ = 8192
NMETA = 16
L = SEQ + NMETA
LH = L // 2
NT = 33
NPAD = NT * 128
DFF = 2816
NKB = 9
EPS = 1e-6

V_G1, V_GP1, V_G2, V_GP2 = 0, 8, 16, 24
V_BG = 32
V_CWM = 48
V_CBM = 60
V_CWF = 64
V_CBF = 196
NV = 240

ENGS = ("sync", "scalar", "vector", "gpsimd", "tensor")


class Op:
    __slots__ = ("eng", "fn", "deps", "is_dma", "sig", "signaling", "dsem", "seq")

    def __init__(self, eng, fn, is_dma, dsem):
        self.eng = eng
        self.fn = fn
        self.deps = []
        self.is_dma = is_dma
        self.sig = None
        self.signaling = is_dma
        self.dsem = dsem


class Prog:
    def __init__(self):
        self.ops = {e: [] for e in ENGS}
        self.last_w = {}
        self.readers = {}
        self.same_engine_sync = {"scalar", "vector", "gpsimd"}
        self.dma_sems = {}
        self.group_sems = set()
        self.nops = 0

    def op(self, eng, fn, reads=(), writes=(), dma=False, dsem=None, group=False):
        o = Op(eng, fn, dma, dsem)
        self.nops += 1
        o.seq = self.nops
        if dma:
            assert self.dma_sems.setdefault(dsem, eng) == eng
            if group:
                self.group_sems.add(dsem)
        deps = {}
        for t in reads:
            w = self.last_w.get(t)
            if w is not None:
                deps[id(w)] = w
        for t in writes:
            w = self.last_w.get(t)
            if w is not None:
                deps[id(w)] = w
            for r in self.readers.get(t, ()):
                deps[id(r)] = r
        best = {}
        for p in deps.values():
            if p is o:
                continue
            if (not p.is_dma) and (not dma) and p.eng == eng and eng not in self.same_engine_sync:
                continue
            key = ("dma", p.dsem) if p.is_dma else ("eng", p.eng)
            q = best.get(key)
            if q is None or p.seq > q.seq:
                best[key] = p
        for p in best.values():
            o.deps.append(p)
            p.signaling = True
        for t in reads:
            self.readers.setdefault(t, []).append(o)
        for t in writes:
            self.last_w[t] = o
            self.readers[t] = []
        self.ops[eng].append(o)
        return o

    def emit(self, nc, final_wait_ops=()):
        dma_cnt = {}
        for e in ENGS:
            c = 0
            for o in self.ops[e]:
                if o.is_dma:
                    dma_cnt[o.dsem] = dma_cnt.get(o.dsem, 0) + 16
                    o.sig = (("dma", o.dsem), dma_cnt[o.dsem])
                elif o.signaling:
                    c += 1
                    o.sig = (("eng", e), c)
        for e in ENGS:
            for o in self.ops[e]:
                if o.is_dma and o.dsem in self.group_sems:
                    o.sig = (("dma", o.dsem), dma_cnt[o.dsem])
        semkeys = [("eng", e) for e in ENGS] + [("dma", d) for d in sorted(self.dma_sems)]
        with contextlib.ExitStack() as st:
            sems = {k: st.enter_context(nc.semaphore(f"s_{k[0]}_{k[1]}")) for k in semkeys}
            block = st.enter_context(nc.Block())
            stats = {}

            def make_body(e):
                def body(engobj):
                    seen = {}
                    nw = 0
                    for o in self.ops[e]:
                        for p in o.deps:
                            k, v = p.sig
                            if seen.get(k, 0) >= v:
                                continue
                            seen[k] = v
                            engobj.wait_ge(sems[k], v)
                            nw += 1
                        inst = o.fn(engobj)
                        if o.signaling:
                            inst.then_inc(sems[o.sig[0]], 16 if o.is_dma else 1)
                    if e == "sync":
                        for o in final_wait_ops:
                            k, v = o.sig
                            if seen.get(k, 0) >= v:
                                continue
                            seen[k] = v
                            engobj.wait_ge(sems[k], v)
                    stats[e] = (len(self.ops[e]), nw)
                return body

            for e in ENGS:
                getattr(block, e)(make_body(e))
            self.stats = stats


def build_program(debug=None):
    nc = bass.Bass("TRN2", target_bir_lowering=False)
    P = Prog()

    def dram_in(name, shape, dt):
        return nc.dram_tensor(name, shape, dt, kind="ExternalInput").ap()

    x_d = dram_in("x", [SEQ, D], F32)
    meta_d = dram_in("meta", [NMETA, D], F32)
    w_in_d = dram_in("w_in", [D, 4096], F32)
    w_f_d = dram_in("w_f", [512, D], F32)
    w_co_d = dram_in("w_co", [512, D], F32)
    w_out_d = dram_in("w_out", [D, D], F32)
    w_up_d = dram_in("w_up", [D, 2 * DFF], F32)
    w_dn_d = dram_in("w_dn", [DFF, D], F32)
    vecs_d = dram_in("vecs", [128, NV], F32)
    ident_d = dram_in("ident", [128, 128], F32)
    cs_d = dram_in("cs128", [128, 256], BF16)
    dft_d = dram_in("dft", [NKB * 11 * 128, 3072], BF16)
    out_d = nc.dram_tensor("out", [SEQ, D], F32, kind="ExternalOutput").ap()

    s_in = nc.dram_tensor("s_in", [D, 4096], BF16, kind="Internal").ap()
    s_f = nc.dram_tensor("s_f", [512, D], BF16, kind="Internal").ap()
    s_co = nc.dram_tensor("s_co", [512, D], BF16, kind="Internal").ap()
    s_out = nc.dram_tensor("s_out", [D, D], BF16, kind="Internal").ap()
    s_up = nc.dram_tensor("s_up", [D, 2 * DFF], BF16, kind="Internal").ap()
    s_dn = nc.dram_tensor("s_dn", [DFF, D], BF16, kind="Internal").ap()

    NSLOT = 3
    with contextlib.ExitStack() as st:
        def sb(name, shape, dt):
            return st.enter_context(nc.sbuf_tensor(name, shape, dt))

        def ps(name, shape, dt):
            return st.enter_context(nc.psum_tensor(name, shape, dt))

        VEC = sb("VEC", [128, NV], F32)
        ID = sb("ID", [128, 128], F32)
        CS = sb("CS", [128, 256], BF16)
        ONES = sb("ONES", [128, 128], BF16)
        EPSC = sb("EPSC", [128, 2], F32)
        DUM = sb("DUM", [128, 16], F32)
        WR = sb("WR", [128, NSLOT, 4096], BF16)
        R1 = sb("R1", [128, 33792], BF16)
        R2 = sb("R2", [128, 33792], BF16)
        RS = sb("RS", [128, 8, 514], F32)
        XS2 = sb("XS", [128, 2, 1024], F32)
        RS2 = sb("RS2", [128, 8, 514], F32)
        SQ = sb("SQ", [128, 2, 512], BF16)
        RSTD = sb("RSTD", [128, 512], F32)
        RT = sb("RT", [128, 512], F32)
        UCAR = sb("UCAR", [128, 44, 2], F32)
        CAR_CV = sb("CAR_CV", [128, 4, 2], F32)
        CAR_BG = sb("CAR_BG", [128, 4, 2], BF16)
        CAR_GA = sb("CAR_GA", [128, 8, 2], BF16)
        CAR_GB = sb("CAR_GB", [128, 8, 2], BF16)

        PB = [ps(f"PB{i}", [128, 512], F32) for i in range(6)]
        TRT = ps("TRT", [128, 1024], F32)
        banks = [(PB[i][:, :], f"P{i}") for i in range(6)] + [(TRT[:, 0:512], "P6"), (TRT[:, 512:1024], "P7")]

        def view(reg, byte_off, dt, shape):
            n = int(np.prod(shape[1:]))
            if dt == F32:
                v = reg[:, byte_off // 2: byte_off // 2 + 2 * n].bitcast(F32)
            else:
                v = reg[:, byte_off // 2: byte_off // 2 + n]
            if len(shape) == 3:
                v = v.rearrange("p (a b) -> p a b", a=shape[1])
            return v

        UT = view(R1, 0, BF16, [128, 4, L])
        UCS = view(R1, 0, BF16, [128, NT, 1024])
        UP = view(R2, 0, BF16, [128, 4, NPAD])
        UM = view(R2, 2 * 4 * NPAD, BF16, [128, 4, NPAD])
        FT = view(R2, 0, BF16, [128, 4, L])
        G = view(R1, 0, BF16, [128, 22, 512])
        GA = view(R1, 0, BF16, [128, 8, 514])
        GB = view(R1, 8224, BF16, [128, 8, 514])
        M = view(R1, 22528, BF16, [128, 8, 512])
        S0 = 30720
        CV = view(R1, S0, F32, [128, 4, 514])
        BG = view(R1, S0 + 8224, BF16, [128, 4, 514])
        CO = view(R1, S0 + 12336, F32, [128, 2, 512])
        Z = view(R1, S0 + 16432, BF16, [128, 4, 512])
        T1 = view(R1, S0 + 20528, F32, [128, 2, 512])
        T2 = view(R1, S0 + 24624, F32, [128, 2, 512])
        MO = view(R1, S0, F32, [128, 8, 512])
        U = view(R1, S0, F32, [128, 6, 516])
        ACC = view(R1, S0 + 12384, F32, [128, 6, 512])
        GL = view(R1, S0 + 24672, F32, [128, 2, 512])
        OSS = [(view(R1, S0 + 16384, F32, [128, 1024]), ["OS0"]), (view(R1, S0 + 20480, F32, [128, 1024]), ["OS1"])]
        SQB = view(R1, 63488, BF16, [128, 2, 512])
        RSTD2 = view(R1, 65536, F32, [128, 512])
        HTM = (M, "M")
        SQS = [SQ, SQB]
        RSTDS = [(RSTD, "RSTD"), (RSTD2, "RSTD2")]
        RT2 = view(R1, S0 + 28768, F32, [128, 512])
        RTS = [(RT, "RT"), (RT2, "RT2")]

        ASB = RS[:, 0:2, 0:512]
        xsconf = {"buf": XS2, "n": 2}
        cnt = {"bank": 0, "slot": 0, "alt": 0, "fence": 0, "xs": 0, "os": 0}

        def act(out, in_, func, reads, writes, scale=1.0, bias=None):
            kw = {} if bias is None else {"bias": bias}
            return P.op("scalar", lambda e: e.activation(out=out, in_=in_, func=func, scale=scale, **kw),
                        reads=reads, writes=writes)

        def stt(out, in0, scalar, in1, op0, op1, reads, writes):
            return P.op("vector", lambda e: e.scalar_tensor_tensor(out=out, in0=in0, scalar=scalar, in1=in1,
                                                                   op0=op0, op1=op1), reads=reads, writes=writes)

        def tt(eng, out, in0, in1, op, reads, writes):
            return P.op(eng, lambda e: e.tensor_tensor(out=out, in0=in0, in1=in1, op=op), reads=reads, writes=writes)

        def cp(eng, out, in_, reads, writes):
            if eng == "scalar":
                return P.op(eng, lambda e: e.copy(out=out, in_=in_), reads=reads, writes=writes)
            return P.op(eng, lambda e: e.tensor_copy(out=out, in_=in_), reads=reads, writes=writes)

        def mset(eng, ap, val, writes, reads=()):
            return P.op(eng, lambda e: e.memset(ap, val), reads=reads, writes=writes)

        def mm(out, lhsT, rhs, start, stop, reads, writes):
            return P.op("tensor", lambda e: e.matmul(out, lhsT=lhsT, rhs=rhs, start=start, stop=stop),
                        reads=reads, writes=writes)

        def tr(out, in_, ident, reads, writes):
            return P.op("tensor", lambda e: e.transpose(out=out, in_=in_, identity=ident), reads=reads, writes=writes)

        def dma(q, out, in_, dsem, reads, writes, group=False):
            return P.op(q, lambda e: e.dma_start(out=out, in_=in_), reads=reads, writes=writes, dma=True,
                        dsem=dsem, group=group)

        def fence(reads):
            cnt["fence"] += 1
            tok = f"fence{cnt['fence']}"
            k = cnt["fence"] % 16
            mset("gpsimd", DUM[:, k:k + 1], 0.0, writes=[tok, f"DUM{k}"] + list(reads))
            return tok

        def next_bank():
            b = cnt["bank"] % 4
            cnt["bank"] += 1
            return banks[b]

        SSBS = [banks[4], banks[5]]
        TRA = (TRT[:, :], ["P6", "P7"])

        def alt_eng():
            cnt["alt"] += 1
            return "scalar" if cnt["alt"] % 2 else "vector"

        dbg_outs = {}

        def dbg(name, ap, shape, dt, reads):
            if debug and name in debug:
                t = nc.dram_tensor("dbg_" + name, shape, dt, kind="ExternalOutput").ap()
                o = dma("sync", t, ap, "dbg_" + name, reads=reads, writes=[])
                dbg_outs[name] = o

        def load_unit(src, kc, ranges, src_tok):
            slot = cnt["slot"] % NSLOT
            cnt["slot"] += 1
            tot = sum(n for _, n in ranges)
            tok = f"W{slot}"
            v = WR[:, slot, 0:kc * tot].rearrange("p (k n) -> p k n", k=kc)
            if len(ranges) == 1:
                c0, n = ranges[0]
                s3 = src.rearrange("(k p) n -> p k n", p=128)
                dma("sync", v, s3[:, :, c0:c0 + n], f"wr{slot}", reads=[src_tok], writes=[tok])
            else:
                (c0, n), (c1, n1) = ranges
                assert n == n1
                s4 = src.rearrange("(k p) (h n) -> p k h n", p=128, h=2)
                assert c1 - c0 == s4.shape[3]
                v4 = WR[:, slot, 0:kc * tot].rearrange("p (k h n) -> p k h n", k=kc, h=2)
                dma("sync", v4, s4[:, :, :, c0:c0 + n], f"wr{slot}", reads=[src_tok], writes=[tok])
            return v, [tok]

        dma("sync", VEC[:, :], vecs_d[:, :], "c0", reads=[], writes=["VEC"])
        dma("sync", ID[:, :], ident_d[:, :], "c1", reads=[], writes=["ID"])
        dma("sync", CS[:, :], cs_d[:, :], "c2", reads=[], writes=["CS"])
        mset("gpsimd", ONES[:, :], 1.0, writes=["ONES"])
        mset("gpsimd", EPSC[:, :], EPS, writes=["EPSC"])
        mset("gpsimd", UCAR[:, :, :], 0.0, writes=[f"UCAR{c}" for c in range(44)])

        def cast_weight(src, dst, rows, cols, cpiece, name):
            for r0 in range(0, rows, 128):
                for c0 in range(0, cols, cpiece):
                    dma("gpsimd", dst[r0:r0 + 128, c0:c0 + cpiece], src[r0:r0 + 128, c0:c0 + cpiece],
                        "cast_" + name, reads=[], writes=["SCR_" + name + f"_{r0}_{c0}"], group=True)
            return "SCR_" + name + f"_{rows - 128}_{cols - cpiece}"

        T_IN0 = cast_weight(w_in_d[:, 0:512], s_in[:, 0:512], D, 512, 512, "in0")
        T_IN = cast_weight(w_in_d[:, 512:4096], s_in[:, 512:4096], D, 3584, 1792, "in")
        T_F = cast_weight(w_f_d, s_f, 512, D, 1024, "f")
        T_CO = cast_weight(w_co_d, s_co, 512, D, 1024, "co")
        T_OUT = cast_weight(w_out_d, s_out, D, D, 1024, "out")
        s_up_r = s_up.rearrange("r (u h j) -> r u h j", h=2, j=256)
        for r0 in range(0, D, 128):
            for h in range(2):
                dma("gpsimd", s_up_r[r0:r0 + 128, :, h, :],
                    w_up_d[r0:r0 + 128, DFF * h:DFF * (h + 1)].rearrange("r (u j) -> r u j", j=256),
                    "cast_up", reads=[], writes=[f"SCR_up_{r0}_{h}"], group=True)
        T_UP = f"SCR_up_{D - 128}_1"
        T_DN = cast_weight(w_dn_d, s_dn, DFF, D, 1024, "dn")

        def load_x_sub(col0, sub, r, ta, rsb):
            rs, rsn = rsb
            XS, nxs = xsconf["buf"], xsconf["n"]
            slot = cnt["xs"] % nxs
            cnt["xs"] += 1
            tok = f"XS{slot}"
            if ta < NMETA:
                nm = min(NMETA - ta, r)
                dma("sync", XS[0:nm, slot, :], meta_d[ta:ta + nm, :], f"xs{slot}", reads=[], writes=[tok])
                if r > nm:
                    dma("sync", XS[nm:r, slot, :], x_d[0:r - nm, :], f"xs{slot}", reads=[], writes=[tok])
            else:
                dma("sync", XS[0:r, slot, :], x_d[ta - NMETA:ta - NMETA + r, :], f"xs{slot}", reads=[], writes=[tok])
            for dc in range(8):
                tr(TRT[:, 128 * dc:128 * dc + r], XS[0:r, slot, 128 * dc:128 * dc + 128], ID[0:r, 0:r],
                   reads=[tok, "ID"], writes=TRA[1])
            src = TRT[:, :].rearrange("p (c n) -> p c n", c=8)[:, :, 0:r]
            cp("scalar", rs[:, :, col0 + 128 * sub:col0 + 128 * sub + r], src, reads=[],
               writes=TRA[1] + [f"{rsn}{dc}" for dc in range(8)])

        def load_x_T(col0, n, t0, rsb):
            for sub in range((n + 127) // 128):
                load_x_sub(col0, sub, min(128, n - 128 * sub), t0 + 128 * sub, rsb)

        def finish_rstd(n, which=0):
            ssb = SSBS[which]
            rstd, rtok = RSTDS[which]
            rt, rttok = RTS[which]
            act(rt[:, 0:n], ssb[0][:, 0:n], AF.Sqrt, reads=["EPSC"], writes=[ssb[1], rttok], scale=1.0 / D,
                bias=EPSC[:, 0:1])
            P.op("vector", lambda e: e.reciprocal(out=rstd[:, 0:n], in_=rt[:, 0:n]), reads=[rttok], writes=[rtok])

        def norm_part1(colA, n, rsb, which=0, extra=()):
            rs, rsn = rsb
            ssb = SSBS[which]
            for dc in range(8):
                sq = SQS[which][:, dc % 2, 0:n]
                sqt = f"SQ{which}{dc % 2}"
                act(sq, rs[:, dc, colA:colA + n], AF.Square, reads=[f"{rsn}{dc}"] + list(extra), writes=[sqt])
                mm(ssb[0][:, 0:n], ONES[:, :], sq, dc == 0, dc == 7, reads=[sqt, "ONES"], writes=[ssb[1]])

        def norm_part2(colA, n, gcol, rsb, htb, which=0, extra=(), do_finish=True):
            rs, rsn = rsb
            ht, htn = htb
            rstd, rtok = RSTDS[which]
            if do_finish:
                finish_rstd(n, which)
            for dc in range(8):
                stt(ht[:, dc, 0:n], rs[:, dc, colA:colA + n], VEC[:, gcol + dc:gcol + dc + 1], rstd[:, 0:n],
                    ALU.mult, ALU.mult, reads=[f"{rsn}{dc}", "VEC", rtok] + list(extra), writes=[f"{htn}{dc}"])

        def norm_to_HT(colA, n, gcol, rsb, htb, which=0, extra=()):
            norm_part1(colA, n, rsb, which, extra)
            norm_part2(colA, n, gcol, rsb, htb, which, extra)

        HT_ALL = [f"HT{dc}" for dc in range(8)]

        WU, WU_T = load_unit(s_in, 8, [(0, 512)], T_IN0)
        HTA = view(R2, 0, BF16, [128, 8, 512])
        HTB = view(R2, 8192, BF16, [128, 8, 512])
        p1buf = [((RS, "RS"), (HTA, "HTa")), ((RS2, "RQ"), (HTB, "HTb"))]
        XS8 = view(R2, 16384, F32, [128, 8, 1024])
        xsconf["buf"], xsconf["n"] = XS8, 8

        def p1_width(tb):
            return 512 if tb < 16 else 16

        sv = (SQS[1], RSTDS[1], RTS[1])
        SQS[1] = view(R2, 49152, BF16, [128, 2, 512])
        RSTDS[1] = (view(R2, 51200, F32, [128, 512]), "RSTD2")
        RTS[1] = (view(R2, 53248, F32, [128, 512]), "RT2")
        load_x_T(2, p1_width(0), 0, rsb=p1buf[0][0])
        norm_part1(2, p1_width(0), p1buf[0][0], 0)
        finish_rstd(p1_width(0), 0)
        for tb in range(17):
            n = p1_width(tb)
            t0 = 512 * tb
            rsb, htb = p1buf[tb % 2]
            if tb + 1 < 17:
                load_x_T(2, p1_width(tb + 1), 512 * (tb + 1), rsb=p1buf[(tb + 1) % 2][0])
            norm_part2(2, n, V_G1, rsb, htb, tb % 2, do_finish=False)
            for g in range(4):
                bk, btok = next_bank()
                for dc in range(8):
                    mm(bk[:, 0:n], WU[:, dc, 128 * g:128 * g + 128], htb[0][:, dc, 0:n], dc == 0, dc == 7,
                       reads=WU_T + [f"{htb[1]}{dc}"], writes=[btok])
                cp(alt_eng(), UT[:, g, t0:t0 + n], bk[:, 0:n], reads=[], writes=[btok, f"UT{g}"])
            if tb + 1 < 17:
                norm_part1(2, p1_width(tb + 1), p1buf[(tb + 1) % 2][0], (tb + 1) % 2)
                finish_rstd(p1_width(tb + 1), (tb + 1) % 2)
        f_p1 = fence([f"HTa{dc}" for dc in range(8)] + [f"HTb{dc}" for dc in range(8)] + [f"XS{i}" for i in range(8)]
                     + ["SQ10", "SQ11", "RSTD2", "RT2"])
        SQS[1], RSTDS[1], RTS[1] = sv
        xsconf["buf"], xsconf["n"] = XS2, 2
        cnt["xs"] = 0
        dbg("ut", UT[:, :, :], [128, 4, L], BF16, reads=[f"UT{g}" for g in range(4)])

        for g in range(4):
            mset("gpsimd", UP[:, g, LH + 1:NPAD], 0.0, writes=[f"UP{g}"], reads=[f_p1])
            mset("gpsimd", UM[:, g, LH:NPAD], 0.0, writes=[f"UM{g}"], reads=[f_p1])
            mset("gpsimd", UM[:, g, 0:1], 0.0, writes=[f"UM{g}"])
            a = UT[:, g, 1:LH]
            b = UT[:, g, L - 1:LH:-1]
            tt("vector", UP[:, g, 1:LH], a, b, ALU.add, reads=[f"UT{g}"], writes=[f"UP{g}"])
            tt("vector", UM[:, g, 1:LH], a, b, ALU.subtract, reads=[f"UT{g}"], writes=[f"UM{g}"])
            cp("gpsimd", UP[:, g, 0:1], UT[:, g, 0:1], reads=[f"UT{g}"], writes=[f"UP{g}"])
            cp("gpsimd", UP[:, g, LH:LH + 1], UT[:, g, LH:LH + 1], reads=[f"UT{g}"], writes=[f"UP{g}"])
        f_r1 = fence([f"UT{g}" for g in range(4)])
        for i in range(NT):
            for g in range(4):
                mm(TRT[:, 256 * g:256 * g + 128], UP[:, g, 128 * i:128 * i + 128], CS[:, 0:128], True, True,
                   reads=[f"UP{g}", "CS"], writes=TRA[1])
                mm(TRT[:, 256 * g + 128:256 * g + 256], UM[:, g, 128 * i:128 * i + 128], CS[:, 128:256], True, True,
                   reads=[f"UM{g}", "CS"], writes=TRA[1])
            cp(alt_eng(), UCS[:, i, :], TRT[:, :], reads=[f_r1], writes=TRA[1] + [f"UCS{i}"])
        dbg("ucs", UCS[:, :, :], [128, NT, 1024], BF16, reads=[f"UCS{i}" for i in range(NT)])
        f_r2 = fence([f"UP{g}" for g in range(4)] + [f"UM{g}" for g in range(4)])

        it = 0
        for gp in range(2):
            for kb in range(NKB):
                wk = 512 if kb < NKB - 1 else LH + 1 - 512 * (NKB - 1)
                k0 = 512 * kb
                bset = [banks[(it % 2) * 4 + q] for q in range(4)]
                it += 1
                for t in range(11):
                    slot = cnt["slot"] % NSLOT
                    cnt["slot"] += 1
                    row0 = (kb * 11 + t) * 128
                    dv = WR[:, slot, 0:3072]
                    dtoks = [f"W{slot}"]
                    dma("sync", dv, dft_d[row0:row0 + 128, :], f"wr{slot}", reads=[], writes=dtoks)
                    dv4 = dv.rearrange("p (j c k) -> p j c k", j=3, c=2)
                    for j in range(3):
                        i = 3 * t + j
                        for gi in range(2):
                            g = 2 * gp + gi
                            mm(bset[2 * gi][0][:, 0:wk], UCS[:, i, 256 * g:256 * g + 128], dv4[:, j, 0, 0:wk],
                               i == 0, i == NT - 1, reads=dtoks + [f"UCS{i}"], writes=[bset[2 * gi][1]])
                            mm(bset[2 * gi + 1][0][:, 0:wk], UCS[:, i, 256 * g + 128:256 * g + 256],
                               dv4[:, j, 1, 0:wk], i == 0, i == NT - 1, reads=dtoks + [f"UCS{i}"],
                               writes=[bset[2 * gi + 1][1]])
                for gi in range(2):
                    g = 2 * gp + gi
                    a_ps, a_tok = bset[2 * gi]
                    b_ps, b_tok = bset[2 * gi + 1]
                    asb = ASB[:, gi, 0:wk]
                    cp("scalar", asb, a_ps[:, 0:wk], reads=[], writes=[a_tok, f"RS{gi}"])
                    tt("vector", FT[:, g, k0:k0 + wk], asb, b_ps[:, 0:wk], ALU.add, reads=[f"RS{gi}", f_r2],
                       writes=[b_tok, f"FT{g}"])
                    lo = 1 if kb == 0 else 0
                    hi = wk if kb < NKB - 1 else wk - 1
                    m = hi - lo
                    dst = FT[:, g, L - (k0 + lo):L - (k0 + lo) - m:-1]
                    tt("vector", dst, asb[:, lo:hi], b_ps[:, lo:hi], ALU.subtract, reads=[f"RS{gi}", f_r2],
                       writes=[b_tok, f"FT{g}"])
        dbg("ft", FT[:, :, :], [128, 4, L], BF16, reads=[f"FT{g}" for g in range(4)])
        f_p2 = fence([f"UCS{i}" for i in range(NT)])
        FT_ALL = [f"FT{g}" for g in range(4)]

        rsbufs = [(RS, "RS"), (RS2, "RQ")]

        def rs_of(j):
            return rsbufs[j % 2]

        CVT = [f"CV{q}" for q in range(4)]
        BGT = [f"BG{q}" for q in range(4)]
        GAT = [f"GA{q}" for q in range(8)]
        GBT = [f"GB{q}" for q in range(8)]
        MOT = [f"MO{q}" for q in range(8)]
        GT = [f"G{c}" for c in range(22)]
        U_VCB = (2, 0, 1)
        U_GATES = (3, 4, 5, 6)

        def A_proj(n, units, fA, after_unit=None):
            HT = M
            for u in units:
                wv, wtok = load_unit(s_in, 8, [(512 + 512 * u, 512)], T_IN)
                for q in range(4):
                    bk, btok = next_bank()
                    for dc in range(8):
                        mm(bk[:, 0:n], wv[:, dc, 128 * q:128 * q + 128], HT[:, dc, 0:n], dc == 0, dc == 7,
                           reads=wtok + [f"M{dc}"], writes=[btok])
                    if u == 2:
                        cp("scalar", CV[:, q, 2:2 + n], bk[:, 0:n], reads=[fA], writes=[btok, f"CV{q}"])
                    elif u == 0:
                        tt("vector", CV[:, q, 2:2 + n], bk[:, 0:n], CV[:, q, 2:2 + n], ALU.mult, reads=[fA],
                           writes=[btok, f"CV{q}"])
                    elif u == 1:
                        cp("scalar", BG[:, q, 2:2 + n], bk[:, 0:n], reads=[fA], writes=[btok, f"BG{q}"])
                    else:
                        gi = (u - 3) * 4 + q
                        dst = GA if gi < 8 else GB
                        nm = "GA" if gi < 8 else "GB"
                        act(dst[:, gi % 8, 2:2 + n], bk[:, 0:n], AF.Sigmoid, reads=[fA, "VEC"],
                            writes=[btok, f"{nm}{gi % 8}"], bias=VEC[:, V_BG + gi:V_BG + gi + 1])
                if after_unit and u in after_unit:
                    after_unit[u]()

        def save_carries(n):
            cp("gpsimd", CAR_CV[:, :, :], CV[:, :, n:n + 2], reads=CVT, writes=["CAR_CV"])
            cp("gpsimd", CAR_BG[:, :, :], BG[:, :, n:n + 2], reads=BGT, writes=["CAR_BG"])
            cp("gpsimd", CAR_GA[:, :, :], GA[:, :, n:n + 2], reads=GAT, writes=["CAR_GA"])
            cp("gpsimd", CAR_GB[:, :, :], GB[:, :, n:n + 2], reads=GBT, writes=["CAR_GB"])

        def restore_gates(fA):
            cp("gpsimd", GA[:, :, 0:2], CAR_GA[:, :, :], reads=["CAR_GA", fA], writes=GAT)
            cp("gpsimd", GB[:, :, 0:2], CAR_GB[:, :, :], reads=["CAR_GB", fA], writes=GBT)

        def restore_vcb(fA):
            cp("gpsimd", CV[:, :, 0:2], CAR_CV[:, :, :], reads=["CAR_CV", fA], writes=CVT)
            cp("gpsimd", BG[:, :, 0:2], CAR_BG[:, :, :], reads=["CAR_BG", fA], writes=BGT)

        def pn_matmuls(n, produce, f_mo, which=0):
            ssb = SSBS[which]
            cntd = 0
            for dco, bk, btok in produce():
                cp("scalar", MO[:, dco, 0:n], bk[:, 0:n], reads=[f_mo], writes=[btok, f"MO{dco}"])
                sq = SQS[which][:, cntd % 2, 0:n]
                sqt = f"SQ{which}{cntd % 2}"
                act(sq, bk[:, 0:n], AF.Square, reads=[], writes=[btok, sqt])
                mm(ssb[0][:, 0:n], ONES[:, :], sq, cntd == 0, cntd == 7, reads=[sqt, "ONES"], writes=[ssb[1]])
                cntd += 1

        def pn_finish(n, gcol, rsb, colR, add_into_rs, which=0):
            rs, rsn = rsb
            rstd, rtok = RSTDS[which]
            finish_rstd(n, which)
            for dc in range(8):
                stt(MO[:, dc, 0:n], MO[:, dc, 0:n], VEC[:, gcol + dc:gcol + dc + 1], rstd[:, 0:n], ALU.mult, ALU.mult,
                    reads=["VEC", rtok], writes=[f"MO{dc}"])
                if add_into_rs:
                    tt("gpsimd", rs[:, dc, colR:colR + n], rs[:, dc, colR:colR + n], MO[:, dc, 0:n], ALU.add,
                       reads=[f"MO{dc}"], writes=[f"{rsn}{dc}"])
                else:
                    tt("gpsimd", MO[:, dc, 0:n], MO[:, dc, 0:n], rs[:, dc, colR:colR + n], ALU.add,
                       reads=[f"{rsn}{dc}"], writes=[f"MO{dc}"])

        CO4 = view(R1, S0 + 20528, F32, [128, 4, 512])
        CO4T = ["T10", "T11", "T20", "T21"]

        def conv_co(nB, fS):
            cwm = lambda k, q: VEC[:, V_CWM + 4 * k + q:V_CWM + 4 * k + q + 1]
            for q in range(4):
                co = CO4[:, q, 0:nB]
                ctok = CO4T[q]
                act(co, CV[:, q, 1:1 + nB], AF.Identity, reads=[f"CV{q}", "VEC", fS], writes=[ctok], scale=cwm(1, q),
                    bias=VEC[:, V_CBM + q:V_CBM + q + 1])
                stt(co, CV[:, q, 0:nB], cwm(0, q), co, ALU.mult, ALU.add, reads=[f"CV{q}", "VEC"], writes=[ctok])
                stt(co, CV[:, q, 2:2 + nB], cwm(2, q), co, ALU.mult, ALU.add, reads=[f"CV{q}", "VEC"], writes=[ctok])

        def stage_B(nB, s, rsb, fS, conv_done=False):
            if not conv_done:
                conv_co(nB, fS)
            for q in range(4):
                tt("gpsimd", Z[:, q, 0:nB], BG[:, q, 1:1 + nB], CO4[:, q, 0:nB], ALU.mult,
                   reads=[f"BG{q}", CO4T[q], fS], writes=[f"Z{q}"])
            tb0 = s - 1
            for h in range(2):
                wco, wco_t = load_unit(s_co, 4, [(512 * h, 512)], T_CO)
                wf, wf_t = load_unit(s_f, 4, [(512 * h, 512)], T_F)
                for dq in range(4):
                    dc = 4 * h + dq
                    bkb, btb = next_bank()
                    for q in range(4):
                        mm(bkb[:, 0:nB], wco[:, q, 128 * dq:128 * dq + 128], Z[:, q, 0:nB], q == 0, q == 3,
                           reads=wco_t + [f"Z{q}"], writes=[btb])
                    bka, bta = next_bank()
                    for g in range(4):
                        mm(bka[:, 0:nB], wf[:, g, 128 * dq:128 * dq + 128], FT[:, g, tb0:tb0 + nB], g == 0, g == 3,
                           reads=wf_t + [f"FT{g}"], writes=[bta])
                    t1 = T1[:, dc % 2, 0:nB]
                    t2 = T2[:, dc % 2, 0:nB]
                    tt("vector", t1, bka[:, 0:nB], GA[:, dc, 1:1 + nB], ALU.mult, reads=[f"GA{dc}", fS],
                       writes=[bta, f"T1{dc % 2}"])
                    tt("vector", t2, bkb[:, 0:nB], GB[:, dc, 1:1 + nB], ALU.mult, reads=[f"GB{dc}", fS],
                       writes=[btb, f"T2{dc % 2}"])
                    tt("gpsimd", M[:, dc, 0:nB], t1, t2, ALU.add, reads=[f"T1{dc % 2}", f"T2{dc % 2}"],
                       writes=[f"M{dc}"])
            f1 = fence(CVT + BGT + GAT + GBT + [f"Z{q}" for q in range(4)] + ["CO0", "CO1", "T10", "T11", "T20", "T21"])

            def produce_out():
                for h in range(2):
                    wv, wtok = load_unit(s_out, 8, [(512 * h, 512)], T_OUT)
                    bks = [next_bank() for _ in range(4)]
                    for k in range(8):
                        for dq in range(4):
                            mm(bks[dq][0][:, 0:nB], wv[:, k, 128 * dq:128 * dq + 128], M[:, k, 0:nB], k == 0, k == 7,
                               reads=wtok + [f"M{k}"], writes=[bks[dq][1]])
                    for dq in range(4):
                        yield 4 * h + dq, bks[dq][0], bks[dq][1]
            pn_matmuls(nB, produce_out, f1, 0)
            pn_finish(nB, V_GP1, rsb, 1, True, 0)
            f2 = fence(MOT)
            return f1, f2

        def stage_CD(nB, nD, f1, f2, epilogue, rsb, hooks):
            HT = M
            norm_to_HT(1, nB, V_G2, rsb, HTM, 0)
            cwf = lambda k, c: VEC[:, V_CWF + 44 * k + c:V_CWF + 44 * k + c + 1]

            def conv_center(c):
                for half in range(2):
                    ch = c + 22 * half
                    ui = (2 * c + half) % 6
                    ut = [f"U{ui}c", f"U{ui}d"]
                    act(ACC[:, ui, 0:nD], U[:, ui, 1:1 + nD], AF.Identity, reads=ut + ["VEC", f2], writes=[f"ACC{ui}"],
                        scale=cwf(1, ch), bias=VEC[:, V_CBF + ch:V_CBF + ch + 1])

            def conv_side(c):
                for half in range(2):
                    ch = c + 22 * half
                    ui = (2 * c + half) % 6
                    ut = [f"U{ui}c", f"U{ui}d"]
                    atok = f"ACC{ui}"
                    acc = ACC[:, ui, 0:nD]
                    stt(acc, U[:, ui, 0:nD], cwf(0, ch), acc, ALU.mult, ALU.add, reads=ut + ["VEC"], writes=[atok])
                    stt(acc, U[:, ui, 2:2 + nD], cwf(2, ch), acc, ALU.mult, ALU.add, reads=ut + ["VEC"], writes=[atok])

            def gelu_mul(c):
                a0 = (2 * c) % 6
                a1 = (2 * c + 1) % 6
                gl = GL[:, c % 2, 0:nD]
                gtok = f"GL{c % 2}"
                act(gl, ACC[:, a0, 0:nD], AF.Gelu_apprx_tanh, reads=[f"ACC{a0}", f2], writes=[gtok])
                tt("gpsimd", G[:, c, 0:nD], gl, ACC[:, a1, 0:nD], ALU.mult, reads=[gtok, f"ACC{a1}", f1],
                   writes=[f"G{c}"])

            def mm_pair(c, wv, wtok, p):
                res = [next_bank(), next_bank()]
                for dc in range(8):
                    for half in range(2):
                        lo = 256 * half + 128 * p
                        mm(res[half][0][:, 0:nB], wv[:, dc, lo:lo + 128], HT[:, dc, 0:nB], dc == 0, dc == 7,
                           reads=wtok + [f"M{dc}"], writes=[res[half][1]])
                return res

            def evac_pair(c, res):
                for half in range(2):
                    ch = c + 22 * half
                    bk, btok = res[half]
                    ui = (2 * c + half) % 6
                    cp("gpsimd", U[:, ui, 0:2], UCAR[:, ch, :], reads=[f"UCAR{ch}", f2], writes=[f"U{ui}c"])
                    if epilogue:
                        mset("gpsimd", U[:, ui, 2 + nB:4 + nB], 0.0, writes=[f"U{ui}d"], reads=[f2])
                    cp("scalar", U[:, ui, 2:2 + nB], bk[:, 0:nB], reads=[f2], writes=[btok, f"U{ui}d"])
                    if not epilogue:
                        cp("gpsimd", UCAR[:, ch, :], U[:, ui, nB:nB + 2], reads=[f"U{ui}d"], writes=[f"UCAR{ch}"])

            for u in range(11):
                wv, wtok = load_unit(s_up, 8, [(512 * u, 512)], T_UP)
                for p in range(2):
                    c = 2 * u + p
                    res = mm_pair(c, wv, wtok, p)
                    if c in hooks:
                        hooks[c]()
                    if c >= 1:
                        conv_center(c - 1)
                        conv_side(c - 1)
                    evac_pair(c, res)
                    if c >= 2:
                        gelu_mul(c - 2)
            conv_center(21)
            conv_side(21)
            gelu_mul(20)
            gelu_mul(21)
            f3 = fence([f"U{i}{x}" for i in range(6) for x in "cd"] + [f"ACC{i}" for i in range(6)] + ["GL0", "GL1"])
            return f3

        def D2a(nD, f3):
            def produce_dn():
                for pq in range(4):
                    bks = [next_bank(), next_bank()]
                    for kh in range(2):
                        wv, wtok = load_unit(s_dn[1408 * kh:1408 * (kh + 1), :], 11, [(256 * pq, 256)], T_DN)
                        for d2 in range(2):
                            for k in range(11):
                                kk = 11 * kh + k
                                mm(bks[d2][0][:, 0:nD], wv[:, k, 128 * d2:128 * d2 + 128], G[:, kk, 0:nD],
                                   kh == 0 and k == 0, kh == 1 and k == 10, reads=wtok + [f"G{kk}"],
                                   writes=[bks[d2][1]])
                    for d2 in range(2):
                        yield 2 * pq + d2, bks[d2][0], bks[d2][1]
            pn_matmuls(nD, produce_dn, f3, 0)

        def out_stage(nD, s, f3):
            tD0 = s - 2
            k0 = max(0, NMETA - tD0)
            while k0 < nD:
                r = min(128, nD - k0)
                for dc in range(8):
                    tr(TRT[0:r, 128 * dc:128 * dc + 128], MO[:, dc, k0:k0 + r], ID[:, :], reads=[f"MO{dc}", "ID"],
                       writes=TRA[1])
                oi = cnt["os"] % 2
                osb, ostoks = OSS[oi]
                cnt["os"] += 1
                cp("scalar", osb[0:r, :], TRT[0:r, :], reads=[f3], writes=TRA[1] + ostoks)
                row0 = tD0 + k0 - NMETA
                o = dma("gpsimd", out_d[row0:row0 + r, :], osb[0:r, :], f"outd{oi}", reads=ostoks, writes=[])
                out_ops.append(o)
                k0 += r

        out_ops = []

        def carry_rs(src, dst, n):
            cp("gpsimd", dst[0][:, :, 0:2], src[0][:, :, n:n + 2], reads=[f"{src[1]}{q}" for q in range(8)],
               writes=[f"{dst[1]}{q}" for q in range(8)])

        rp = rs_of(-1)
        load_x_T(2, 2, 14, rp)
        norm_to_HT(2, 2, V_G1, rp, HTM, 0, extra=[f_p2])
        A_proj(2, U_VCB + U_GATES, f_p2)
        save_carries(2)
        carry_rs(rp, rs_of(0), 2)
        load_x_T(2, 512, NMETA, rs_of(0))
        norm_to_HT(2, 512, V_G1, rs_of(0), HTM, 0)
        restore_gates(f_p2)
        restore_vcb(f_p2)
        A_proj(512, U_VCB + U_GATES, f_p2)
        save_carries(512)
        fS = f_p2
        for j in range(16):
            s = NMETA + 512 * j
            rsb = rs_of(j)
            nxt = j < 15
            f1, f2 = stage_B(512, s, rsb, fS, conv_done=(j > 0))
            hooks = {}
            if nxt:
                for k in range(4):
                    hooks[3 + 4 * k] = (lambda k=k, j=j: load_x_sub(2, k, 128, NMETA + 512 * (j + 1) + 128 * k,
                                                                    rs_of(j + 1)))
                hooks[19] = (lambda j=j: norm_part1(2, 512, rs_of(j + 1), 1))
            f3 = stage_CD(512, 512, f1, f2, False, rsb, hooks)
            carry_rs(rsb, rs_of(j + 1), 512)
            if nxt:
                norm_part2(2, 512, V_G1, rs_of(j + 1), HTM, 1)
            D2a(512, f3)
            fG = fence(GT)
            if nxt:
                restore_gates(fG)
            pn_finish(512, V_GP2, rsb, 0, False, 0)
            if nxt:
                A_proj(512, U_GATES, fG)
            out_stage(512, s, f3)
            fS = fence(MOT + ["OS0", "OS1"])
            if nxt:
                restore_vcb(fS)
                A_proj(512, U_VCB, fS, after_unit={0: (lambda fS=fS: conv_co(512, fS))})
                save_carries(512)
            if j == 0:
                dbg("rs0", RS[:, :, :], [128, 8, 514], F32, reads=[f"RS{q}" for q in range(8)])
        s = L
        rsb = rs_of(16)
        restore_gates(fS)
        restore_vcb(fS)
        for q in range(4):
            mset("gpsimd", CV[:, q, 2:3], 0.0, writes=[f"CV{q}"])
        f1, f2 = stage_B(1, s, rsb, fS)
        f3 = stage_CD(1, 2, f1, f2, True, rsb, {})
        D2a(2, f3)
        pn_finish(2, V_GP2, rsb, 0, False, 0)
        out_stage(2, s, f3)

        finals = list(out_ops[-2:]) + list(dbg_outs.values())
        P.emit(nc, final_wait_ops=finals)
    return nc, P


_CACHE = {}


def _consts():
    if "c" in _CACHE:
        return _CACHE["c"]
    bf = ml_dtypes.bfloat16
    c = np.arange(128)
    ang = 2 * np.pi * np.outer(c, c) / 128.0
    cs = np.concatenate([np.cos(ang), np.sin(ang)], axis=1) / np.sqrt(128.0)
    cs128 = cs.astype(np.float32).astype(bf)
    ident = np.eye(128, dtype=np.float32)
    n = np.arange(NPAD, dtype=np.int64)[:, None]
    k = np.arange(NKB * 512, dtype=np.int64)[None, :]
    m = (n * k) % L
    angp = (2 * np.pi / L) * m.astype(np.float64)
    valid = ((n <= LH) & (k <= LH))
    sc = 1.0 / np.sqrt(float(L))
    cm = np.where(valid, np.cos(angp) * sc, 0.0).astype(np.float32).astype(bf)
    sm = np.where(valid, -np.sin(angp) * sc, 0.0).astype(np.float32).astype(bf)
    del angp, m
    cm6 = cm.reshape(11, 3, 128, NKB, 512)
    sm6 = sm.reshape(11, 3, 128, NKB, 512)
    both = np.stack([cm6, sm6], axis=0)
    dft = np.ascontiguousarray(both.transpose(4, 1, 3, 2, 0, 5)).reshape(NKB * 11 * 128, 3072)
    _CACHE["c"] = (cs128, ident, dft)
    return _CACHE["c"]


def _vecs(g_mix_pre, g_mix_post, g_ffn_pre, g_ffn_post, b_gates, conv_w_mix, conv_b_mix, conv_w_ffn, conv_b_ffn):
    v = np.zeros((128, NV), np.float32)

    def colmajor(a):
        return np.asarray(a, np.float32).reshape(-1, 128).T

    v[:, V_G1:V_G1 + 8] = colmajor(g_mix_pre[0])
    v[:, V_GP1:V_GP1 + 8] = colmajor(g_mix_post[0])
    v[:, V_G2:V_G2 + 8] = colmajor(g_ffn_pre[0])
    v[:, V_GP2:V_GP2 + 8] = colmajor(g_ffn_post[0])
    v[:, V_BG:V_BG + 16] = colmajor(b_gates[0])
    for k in range(3):
        v[:, V_CWM + 4 * k:V_CWM + 4 * k + 4] = colmajor(conv_w_mix[0, k])
        v[:, V_CWF + 44 * k:V_CWF + 44 * k + 44] = colmajor(conv_w_ffn[0, k])
    v[:, V_CBM:V_CBM + 4] = colmajor(conv_b_mix[0])
    v[:, V_CBF:V_CBF + 44] = colmajor(conv_b_ffn[0])
    return v


def make_in_maps(inputs):
    cs128, ident, dft = _consts()
    f = lambda a: np.ascontiguousarray(np.asarray(a, dtype=np.float32))
    vecs = _vecs(*[np.asarray(inputs[k], np.float32) for k in
                   ("g_mix_pre", "g_mix_post", "g_ffn_pre", "g_ffn_post", "b_gates", "conv_w_mix", "conv_b_mix",
                    "conv_w_ffn", "conv_b_ffn")])
    shared = {
        "meta": f(inputs["meta_tokens"]),
        "w_in": f(inputs["w_in"][0]),
        "w_f": f(inputs["w_fourier"][0]),
        "w_co": f(inputs["w_conv_out"][0]),
        "w_out": f(inputs["w_out"][0]),
        "w_up": f(inputs["w_up"][0]),
        "w_dn": f(inputs["w_down"][0]),
        "vecs": vecs,
        "ident": ident,
        "cs128": cs128,
        "dft": dft,
    }
    x = np.asarray(inputs["x"], dtype=np.float32)
    maps = []
    for b in range(x.shape[0]):
        m = dict(shared)
        m["x"] = np.ascontiguousarray(x[b])
        maps.append(m)
    return maps


def kernel(**inputs):
    if "nc" not in _CACHE:
        _CACHE["nc"] = build_program()[0]
    nc = _CACHE["nc"]
    in_maps = make_in_maps(inputs)
    res = run_bass_kernel_spmd(nc, in_maps, core_ids=list(range(len(in_maps))))
    out = np.stack([np.asarray(r["out"], dtype=np.float32) for r in res.results], axis=0)
    return out
```
